# Optimizing a Trainium2 kernel written in Bass

```python
import jax, jax.numpy as jnp
from jax import lax
import numpy as np

D_MODEL = 1024
BATCH = 4
SEQ = 4096
DEPTH = 1

GRID_W = 64
CTX_LEN = 256
D_MIX = D_MODEL
GDN_WIDTH = D_MIX // 2
GDN_HEADS = 4
GDN_DK = GDN_WIDTH // GDN_HEADS
GDN_DV = GDN_WIDTH // GDN_HEADS
GDN_CHUNK = 64
CONV_W = 4
CONV_PAD_L = 2
CONV_PAD_R = 1
LRU_WIDTH = D_MIX - GDN_WIDTH
LRU_BLOCKS = 8
LRU_BW = LRU_WIDTH // LRU_BLOCKS
LRU_C = 8.0
D_FF = 2816
N_MOD = 9
FFN_RESIDUAL = 0.5
EPS = 1e-6

OFF_Z = 3 * GDN_WIDTH
OFF_BA = 4 * GDN_WIDTH
OFF_RX = OFF_BA + 4 * GDN_HEADS
OFF_RG = OFF_RX + LRU_WIDTH
IN_COLS = OFF_RG + LRU_WIDTH

kernel_name = "hybrid_gdn_rglru_macaron_dit_layer"


def rms_norm(x, g):
    xf = x.astype(jnp.float32)
    y = xf * lax.rsqrt(jnp.mean(xf * xf, axis=-1, keepdims=True) + EPS)
    return (y * g.astype(jnp.float32)).astype(x.dtype)


def modulate(h, shift, scale):
    return h * (1.0 + scale) + shift


def adaln(cvec, w_ada, b_ada):
    m = jax.nn.silu(cvec) @ w_ada + b_ada
    return jnp.split(m[:, None, :], N_MOD, axis=-1)


def ffn_sublayer(h, mods, g, w1, w3, w2):
    shift, scale, gate = mods
    u = modulate(rms_norm(h, g), shift, scale)
    return h + FFN_RESIDUAL * gate * ((jax.nn.silu(u @ w1) * (u @ w3)) @ w2)


def centred_dwconv(x, w, b=None):
    T = x.shape[1]
    xp = jnp.pad(x, ((0, 0), (CONV_PAD_L, CONV_PAD_R), (0, 0)))
    y = xp[:, 0:T] * w[0]
    for k in range(1, CONV_W):
        y = y + xp[:, k:k + T] * w[k]
    return y if b is None else y + b


def l2norm(t):
    return t * lax.rsqrt(jnp.sum(t * t, axis=-1, keepdims=True) + EPS)


def _ident(t):
    return t


def _flip(t):
    return jnp.flip(t, axis=1)


def to_column_major(t):
    B, T, C = t.shape
    rows = T // GRID_W
    return t.reshape(B, rows, GRID_W, C).transpose(0, 2, 1, 3).reshape(B, T, C)


def from_column_major(t):
    B, T, C = t.shape
    rows = T // GRID_W
    return t.reshape(B, GRID_W, rows, C).transpose(0, 2, 1, 3).reshape(B, T, C)


def gdn_chunk_scan(q, k, v, g, beta, s0, with_output):
    B, T, H, DK = q.shape
    DV = v.shape[-1]
    N = T // GDN_CHUNK

    def chunks(t):
        t = t.reshape((B, N, GDN_CHUNK, H) + t.shape[3:])
        return jnp.moveaxis(t, (1, 3), (0, 2))

    qc, kc, vc, gc, bc = chunks(q), chunks(k), chunks(v), chunks(g), chunks(beta)
    gcum = jnp.cumsum(gc, axis=-1)
    idx = jnp.arange(GDN_CHUNK)
    incl = idx[:, None] >= idx[None, :]
    strict = idx[:, None] > idx[None, :]
    diff = gcum[..., :, None] - gcum[..., None, :]
    decay = jnp.exp(jnp.where(incl, diff, -jnp.inf))
    kb = kc * bc[..., None]
    kk = jnp.einsum('nbhid,nbhjd->nbhij', kb, kc)
    a_mat = jnp.where(strict, kk * decay, 0.0) + jnp.eye(GDN_CHUNK, dtype=jnp.float32)
    rhs = jnp.concatenate([vc * bc[..., None], kb * jnp.exp(gcum)[..., None]], axis=-1)
    sol = lax.linalg.triangular_solve(a_mat, rhs, left_side=True, lower=True, unit_diagonal=True)
    u, w = sol[..., :DV], sol[..., DV:]

    if with_output:
        attn = jnp.where(incl, jnp.einsum('nbhid,nbhjd->nbhij', qc, kc) * decay, 0.0)

        def step_out(s, inp):
            q_n, k_n, u_n, w_n, g_n, attn_n = inp
            v_new = u_n - jnp.einsum('bhcd,bhde->bhce', w_n, s)
            g_last = g_n[..., -1:]
            out = (jnp.einsum('bhcd,bhde->bhce', q_n * jnp.exp(g_n)[..., None], s)
                   + jnp.einsum('bhij,bhje->bhie', attn_n, v_new))
            k_dec = k_n * jnp.exp(g_last - g_n)[..., None]
            s = s * jnp.exp(g_last)[..., None] + jnp.einsum('bhcd,bhce->bhde', k_dec, v_new)
            return s, out

        s_fin, out = lax.scan(step_out, s0, (qc, kc, u, w, gcum, attn))
        out = jnp.moveaxis(out, (0, 2), (1, 3)).reshape(B, T, H, DV)
        return out, s_fin

    def step_state(s, inp):
        k_n, u_n, w_n, g_n = inp
        v_new = u_n - jnp.einsum('bhcd,bhde->bhce', w_n, s)
        g_last = g_n[..., -1:]
        k_dec = k_n * jnp.exp(g_last - g_n)[..., None]
        s = s * jnp.exp(g_last)[..., None] + jnp.einsum('bhcd,bhce->bhde', k_dec, v_new)
        return s, None

    s_fin, _ = lax.scan(step_state, s0, (kc, u, w, gcum))
    return None, s_fin


def gdn_prepare(p, conv_w, a_log, dt_bias):
    B, T, _ = p.shape
    qkv = jax.nn.silu(centred_dwconv(p[..., :OFF_Z], conv_w)).astype(jnp.float32)
    q, k, v = jnp.split(qkv, 3, axis=-1)
    q = l2norm(q.reshape(B, T, GDN_HEADS, GDN_DK)) * (GDN_DK ** -0.5)
    k = l2norm(k.reshape(B, T, GDN_HEADS, GDN_DK))
    v = v.reshape(B, T, GDN_HEADS, GDN_DV)
    z = p[..., OFF_Z:OFF_BA]
    ba = p[..., OFF_BA:OFF_RX].astype(jnp.float32).reshape(B, T, 4, GDN_HEADS)
    beta = jax.nn.sigmoid(ba[:, :, 0:2])
    g = -jnp.exp(a_log.astype(jnp.float32)) * jax.nn.softplus(ba[:, :, 2:4] + dt_bias)
    return q, k, v, z, beta, g


def gdn_gated_norm(o, z, norm_w):
    B, T = o.shape[:2]
    zf = z.astype(jnp.float32).reshape(B, T, GDN_HEADS, GDN_DV)
    y = o * lax.rsqrt(jnp.mean(o * o, axis=-1, keepdims=True) + EPS) * norm_w.astype(jnp.float32) * jax.nn.silu(zf)
    return y.reshape(B, T, GDN_WIDTH)


def gdn_group(p_lat, p_ctx, conv_w, a_log, dt_bias, norm_w, with_ctx_out):
    q_l, k_l, v_l, z_l, beta_l, g_l = gdn_prepare(p_lat, conv_w, a_log, dt_bias)
    q_c, k_c, v_c, z_c, beta_c, g_c = gdn_prepare(p_ctx, conv_w, a_log, dt_bias)
    B = p_lat.shape[0]
    o_l, o_c = None, None
    for d in range(2):
        f = _flip if d else _ident
        s0 = jnp.zeros((B, GDN_HEADS, GDN_DK, GDN_DV), jnp.float32)
        oc, s_ctx = gdn_chunk_scan(f(q_c), f(k_c), f(v_c), f(g_c[:, :, d]), f(beta_c[:, :, d]), s0, with_ctx_out)
        ol, _ = gdn_chunk_scan(f(q_l), f(k_l), f(v_l), f(g_l[:, :, d]), f(beta_l[:, :, d]), s_ctx, True)
        o_l = f(ol) if o_l is None else o_l + f(ol)
        if with_ctx_out:
            o_c = f(oc) if o_c is None else o_c + f(oc)
    y_l = gdn_gated_norm(o_l, z_l, norm_w)
    y_c = gdn_gated_norm(o_c, z_c, norm_w) if with_ctx_out else None
    return y_l, y_c


def _lin_combine(e1, e2):
    a1, b1 = e1
    a2, b2 = e2
    return a1 * a2, a2 * b1 + b2


def rglru_scan(xs, w_gate, b_gate, lam, h0, reset_first):
    B, T, C = xs.shape
    xg = xs.reshape(B, T, LRU_BLOCKS, LRU_BW)
    gates = jax.nn.sigmoid(jnp.einsum('btnc,gncd->gbtnd', xg, w_gate.astype(jnp.float32)).reshape(2, B, T, C)
                           + b_gate.astype(jnp.float32)[:, None, None, :])
    r, i = gates[0], gates[1]
    log_a = -LRU_C * jax.nn.softplus(-lam.astype(jnp.float32)) * r
    a = jnp.exp(log_a)
    mult = jnp.sqrt(-jnp.expm1(2.0 * log_a))
    if reset_first:
        mult = mult.at[:, 0].set(1.0)
    b = mult * (i * xs)
    a_cum, h = lax.associative_scan(_lin_combine, (a, b), axis=1)
    return h + a_cum * h0[:, None, :]


def lru_group(p_lat, p_ctx, conv_w, conv_b, w_gate, b_gate, lam, with_ctx_out):
    x_l = centred_dwconv(to_column_major(p_lat[..., OFF_RX:OFF_RG]), conv_w, conv_b).astype(jnp.float32)
    x_c = centred_dwconv(p_ctx[..., OFF_RX:OFF_RG], conv_w, conv_b).astype(jnp.float32)
    B = p_lat.shape[0]
    h_l, h_c = None, None
    for d in range(2):
        f = _flip if d else _ident
        hc = rglru_scan(f(x_c), w_gate[d], b_gate[d], lam[d], jnp.zeros((B, LRU_WIDTH), jnp.float32), True)
        hl = rglru_scan(f(x_l), w_gate[d], b_gate[d], lam[d], hc[:, -1], False)
        h_l = f(hl) if h_l is None else h_l + f(hl)
        if with_ctx_out:
            h_c = f(hc) if h_c is None else h_c + f(hc)
    y_l = from_column_major(h_l) * jax.nn.gelu(p_lat[..., OFF_RG:].astype(jnp.float32))
    y_c = h_c * jax.nn.gelu(p_ctx[..., OFF_RG:].astype(jnp.float32)) if with_ctx_out else None
    return y_l, y_c


def token_mixing(p_lat, p_ctx, w_out, gdn_conv_w, gdn_a_log, gdn_dt_bias, gdn_norm_w,
                 lru_conv_w, lru_conv_b, lru_w_gate, lru_b_gate, lru_lambda, with_ctx_out):
    dt = p_lat.dtype
    a_l, a_c = gdn_group(p_lat, p_ctx, gdn_conv_w, gdn_a_log, gdn_dt_bias, gdn_norm_w, with_ctx_out)
    b_l, b_c = lru_group(p_lat, p_ctx, lru_conv_w, lru_conv_b, lru_w_gate, lru_b_gate, lru_lambda, with_ctx_out)
    y_l = jnp.concatenate([a_l, b_l], axis=-1).astype(dt) @ w_out
    y_c = jnp.concatenate([a_c, b_c], axis=-1).astype(dt) @ w_out if with_ctx_out else None
    return y_l, y_c


def setup_inputs(seed: int = 0) -> dict:
    key = jax.random.key(seed)
    ks = jax.random.split(key, 24)
    nrm = jax.random.normal
    f32 = jnp.float32
    x = nrm(ks[0], (BATCH, SEQ, D_MODEL), f32)
    c = nrm(ks[1], (BATCH, D_MODEL), f32)
    ctx = nrm(ks[2], (BATCH, CTX_LEN, D_MODEL), f32)
    c_ctx = nrm(ks[3], (D_MODEL,), f32)
    w_ada = nrm(ks[4], (DEPTH, D_MODEL, N_MOD * D_MODEL), f32) * (0.5 * D_MODEL ** -0.5)
    b_ada = nrm(ks[5], (DEPTH, N_MOD * D_MODEL), f32) * 0.01
    norm_g = 1.0 + 0.1 * nrm(ks[6], (DEPTH, 3, D_MODEL), f32)
    ffn_w1 = nrm(ks[7], (DEPTH, 2, D_MODEL, D_FF), f32) * D_MODEL ** -0.5
    ffn_w3 = nrm(ks[8], (DEPTH, 2, D_MODEL, D_FF), f32) * D_MODEL ** -0.5
    ffn_w2 = nrm(ks[9], (DEPTH, 2, D_FF, D_MODEL), f32) * D_FF ** -0.5
    w_in = nrm(ks[10], (DEPTH, D_MODEL, IN_COLS), f32) * D_MODEL ** -0.5
    w_out = nrm(ks[11], (DEPTH, D_MIX, D_MODEL), f32) * D_MIX ** -0.5
    gdn_conv_w = nrm(ks[12], (DEPTH, CONV_W, 3 * GDN_WIDTH), f32) * CONV_W ** -0.5
    gdn_a_log = jnp.log(jax.random.uniform(ks[13], (DEPTH, 2, GDN_HEADS), f32, 1.0, 16.0))
    dt = jnp.exp(jax.random.uniform(ks[14], (DEPTH, 2, GDN_HEADS), f32, np.log(1e-3), np.log(1e-1)))
    gdn_dt_bias = dt + jnp.log(-jnp.expm1(-dt))
    gdn_norm_w = 1.0 + 0.1 * nrm(ks[15], (DEPTH, GDN_DV), f32)
    lru_conv_w = nrm(ks[16], (DEPTH, CONV_W, LRU_WIDTH), f32) * CONV_W ** -0.5
    lru_conv_b = nrm(ks[17], (DEPTH, LRU_WIDTH), f32) * 0.01
    lru_w_gate = nrm(ks[18], (DEPTH, 2, 2, LRU_BLOCKS, LRU_BW, LRU_BW), f32) * LRU_BW ** -0.5
    lru_b_gate = nrm(ks[19], (DEPTH, 2, 2, LRU_WIDTH), f32) * 0.1
    a_pow = jax.random.uniform(ks[20], (DEPTH, 2, LRU_WIDTH), f32, 0.9, 0.999) ** (1.0 / LRU_C)
    lru_lambda = jnp.log(a_pow) - jnp.log1p(-a_pow)
    final_norm_g = 1.0 + 0.1 * nrm(ks[21], (D_MODEL,), f32)
    return {"x": x, "c": c, "ctx": ctx, "c_ctx": c_ctx, "w_ada": w_ada, "b_ada": b_ada,
            "norm_g": norm_g, "ffn_w1": ffn_w1, "ffn_w3": ffn_w3, "ffn_w2": ffn_w2,
            "w_in": w_in, "w_out": w_out, "gdn_conv_w": gdn_conv_w, "gdn_a_log": gdn_a_log,
            "gdn_dt_bias": gdn_dt_bias, "gdn_norm_w": gdn_norm_w, "lru_conv_w": lru_conv_w,
            "lru_conv_b": lru_conv_b, "lru_w_gate": lru_w_gate, "lru_b_gate": lru_b_gate,
            "lru_lambda": lru_lambda, "final_norm_g": final_norm_g}


def reference(x, c, ctx, c_ctx, w_ada, b_ada, norm_g, ffn_w1, ffn_w3, ffn_w2, w_in, w_out,
              gdn_conv_w, gdn_a_log, gdn_dt_bias, gdn_norm_w, lru_conv_w, lru_conv_b,
              lru_w_gate, lru_b_gate, lru_lambda, final_norm_g):
    h_lat, h_ctx = x, ctx
    for l in range(DEPTH):
        last = l == DEPTH - 1
        m_lat = adaln(c, w_ada[l], b_ada[l])
        m_ctx = adaln(c_ctx[None, :], w_ada[l], b_ada[l])
        h_lat = ffn_sublayer(h_lat, m_lat[0:3], norm_g[l, 0], ffn_w1[l, 0], ffn_w3[l, 0], ffn_w2[l, 0])
        h_ctx = ffn_sublayer(h_ctx, m_ctx[0:3], norm_g[l, 0], ffn_w1[l, 0], ffn_w3[l, 0], ffn_w2[l, 0])
        p_lat = modulate(rms_norm(h_lat, norm_g[l, 1]), m_lat[3], m_lat[4]) @ w_in[l]
        p_ctx = modulate(rms_norm(h_ctx, norm_g[l, 1]), m_ctx[3], m_ctx[4]) @ w_in[l]
        y_lat, y_ctx = token_mixing(p_lat, p_ctx, w_out[l], gdn_conv_w[l], gdn_a_log[l], gdn_dt_bias[l],
                                    gdn_norm_w[l], lru_conv_w[l], lru_conv_b[l], lru_w_gate[l],
                                    lru_b_gate[l], lru_lambda[l], not last)
        h_lat = h_lat + m_lat[5] * y_lat
        h_lat = ffn_sublayer(h_lat, m_lat[6:9], norm_g[l, 2], ffn_w1[l, 1], ffn_w3[l, 1], ffn_w2[l, 1])
        if not last:
            h_ctx = h_ctx + m_ctx[5] * y_ctx
            h_ctx = ffn_sublayer(h_ctx, m_ctx[6:9], norm_g[l, 2], ffn_w1[l, 1], ffn_w3[l, 1], ffn_w2[l, 1])
    return rms_norm(h_lat, final_norm_g)
```

```python
import numpy as np
from contextlib import ExitStack
import concourse.bass as bass
import concourse.mybir as mybir
from concourse.bass_utils import run_bass_kernel_spmd

F32 = mybir.dt.float32
BF16 = mybir.dt.bfloat16
AF = mybir.ActivationFunctionType
ALU = mybir.AluOpType
ENGS = ["tensor", "vector", "scalar", "gpsimd", "sync"]

D = 1024
KC = 8
FF = 2816
FC = 22
SEQ = 4096
CTX = 256
TALL = CTX + SEQ
OWN = 2048
NMOD = 9
EPS = 1e-6
NT = 256
IN_COLS = 3088
TP = TALL + 8


def pcol(t):
    return 2 + t if t < CTX else 6 + t


class Prog:
    def __init__(self, nc, es):
        self.nc = nc
        self.es = es
        self.q = {e: [] for e in ENGS}
        self.cnt = {}
        self.sems = {}
        self.last_write = {}
        self.readers = {}
        self.knows = {e: {} for e in ENGS}
        self.nops = 0
        self._psum = []
        self._psi = 0
        self._pcnt = {}
        self.st = None

    def sem(self, key):
        if key not in self.sems:
            self.sems[key] = self.es.enter_context(self.nc.semaphore("s_" + key))
            self.cnt[key] = 0
        return self.sems[key]

    def gsb(self, name, shape, dt=F32):
        return self.es.enter_context(self.nc.sbuf_tensor(name, list(shape), dt))

    def sb(self, name, shape, dt=F32):
        return self.st.enter_context(self.nc.sbuf_tensor(name, list(shape), dt))

    def psum_pool(self, n=8, nb=0):
        self._psum = [self.es.enter_context(self.nc.psum_tensor("psb%d" % i, [128, 512], F32)) for i in range(n)]
        self._psumb = [self.es.enter_context(self.nc.psum_tensor("psbf%d" % i, [128, 1024], BF16)) for i in range(nb)]
        self._psbi = 0

    def psumb(self):
        t = self._psumb[self._psbi % len(self._psumb)]
        self._psbi += 1
        return t

    def psum(self, pool=None):
        if pool is None:
            t = self._psum[self._psi % len(self._psum)]
            self._psi += 1
            return t
        idx = {"a": [0, 1, 2, 3], "b": [4, 5]}[pool]
        c = self._pcnt.get(pool, 0)
        self._pcnt[pool] = c + 1
        return self._psum[idx[c % len(idx)]]

    def _need(self, eng, waits, tok, same_ok):
        semkey, val, peng = tok
        if peng == eng and same_ok:
            return
        if peng == "dma":
            val = self.cnt[semkey]
        if self.knows[eng].get(semkey, 0) >= val:
            return
        waits[semkey] = max(waits.get(semkey, 0), val)

    def _key(self, k):
        if isinstance(k, (str, tuple)):
            return k
        return k.name

    def op(self, eng, fn, r=(), w=(), lane=None):
        r = [self._key(k) for k in r]
        w = [self._key(k) for k in w]
        waits = {}
        is_dma = lane is not None
        for k in r:
            if k in self.last_write:
                self._need(eng, waits, self.last_write[k], same_ok=(eng == "tensor" and not is_dma))
            if isinstance(k, str) and k.startswith("psb"):
                for tok in self.readers.get(k, ()):
                    self._need(eng, waits, tok, same_ok=True)
        for k in w:
            if k in self.last_write:
                self._need(eng, waits, self.last_write[k], same_ok=not is_dma)
            for tok in self.readers.get(k, ()):
                self._need(eng, waits, tok, same_ok=not is_dma)
        for sk, v in waits.items():
            self.knows[eng][sk] = v
        if is_dma:
            self.sem(lane)
            self.cnt[lane] += 16
            tok = (lane, self.cnt[lane], "dma")
        else:
            self.sem(eng)
            self.cnt[eng] += 1
            tok = (eng, self.cnt[eng], eng)
        for k in w:
            self.last_write[k] = tok
            self.readers[k] = []
        for k in r:
            self.readers.setdefault(k, []).append(tok)
        self.q[eng].append((waits, fn, tok, is_dma))
        self.nops += 1
        return tok

    def finish(self, eng="sync", skip=()):
        waits = {}
        for sk, c in self.cnt.items():
            if sk in skip:
                continue
            if c > 0 and self.knows[eng].get(sk, 0) < c:
                waits[sk] = c
                self.knows[eng][sk] = c
        self.q[eng].append((waits, None, None, False))

    def emit(self):
        nc = self.nc
        q = self.q
        self.q = {e: [] for e in ENGS}
        with nc.Block() as block:
            def mk(ename):
                def body(e):
                    for waits, fn, tok, is_dma in q[ename]:
                        for sk, v in waits.items():
                            e.wait_ge(self.sems[sk], v)
                        if fn is None:
                            continue
                        ins = fn(e)
                        ins.then_inc(self.sems[tok[0]], 16 if is_dma else 1)
                return body
            block.tensor(mk("tensor"))
            block.vector(mk("vector"))
            block.scalar(mk("scalar"))
            block.gpsimd(mk("gpsimd"))
            block.sync(mk("sync"))

    def mm(self, out, lhsT, rhs, start, stop, r, w):
        return self.op("tensor", lambda e: e.matmul(out, lhsT=lhsT, rhs=rhs, start=start, stop=stop), r=r, w=w)

    def tr(self, out, in_, ident, r, w):
        return self.op("tensor", lambda e: e.transpose(out, in_, ident), r=r, w=w)

    def act(self, out, in_, func, r, w, scale=None, bias=None, accum_out=None, eng="scalar"):
        kw = {}
        if scale is not None:
            kw["scale"] = scale
        if bias is not None:
            kw["bias"] = bias
        if accum_out is not None:
            kw["accum_out"] = accum_out
        return self.op("scalar", lambda e: e.activation(out=out, in_=in_, func=func, **kw), r=r, w=w)

    def tt(self, out, in0, in1, op, r, w, eng="vector"):
        return self.op(eng, lambda e: e.tensor_tensor(out=out, in0=in0, in1=in1, op=op), r=r, w=w)

    def ts(self, out, in0, s1, op0, r, w, s2=None, op1=None, eng="vector"):
        if op1 is None:
            return self.op(eng, lambda e: e.tensor_scalar(out=out, in0=in0, scalar1=s1, scalar2=None, op0=op0), r=r, w=w)
        return self.op(eng, lambda e: e.tensor_scalar(out=out, in0=in0, scalar1=s1, scalar2=s2, op0=op0, op1=op1), r=r, w=w)

    def stt(self, out, in0, scalar, in1, op0, op1, r, w):
        return self.op("vector", lambda e: e.scalar_tensor_tensor(out=out, in0=in0, scalar=scalar, in1=in1, op0=op0, op1=op1), r=r, w=w)

    def cp(self, out, in_, r, w, eng="vector"):
        if eng == "scalar":
            return self.op("scalar", lambda e: e.copy(out=out, in_=in_), r=r, w=w)
        return self.op(eng, lambda e: e.tensor_copy(out=out, in_=in_), r=r, w=w)

    def memset(self, ap, val, w, eng="gpsimd"):
        return self.op(eng, lambda e: e.memset(ap, val), w=w)

    def dma(self, out, in_, r, w, lane, eng="sync"):
        return self.op(eng, lambda e: e.dma_start(out=out, in_=in_), r=r, w=w, lane=lane)


def bcast_rows(t_ap, n_part, offset, length):
    return bass.AP(t_ap.tensor, offset, [[0, n_part], [1, length]])


def build_nc(debug=None, stop_after=None):
    nc = bass.Bass("TRN2", target_bir_lowering=False)
    I = {}

    def inp(name, shape):
        I[name] = nc.dram_tensor(name, list(shape), F32, kind="ExternalInput").ap()
        return I[name]

    x = inp("x", [SEQ, D])
    ctx = inp("ctx", [CTX, D])
    cc = inp("cc", [16, 128])
    w_ada = inp("w_ada", [D, NMOD * D])
    b_ada = inp("b_ada", [1, NMOD * D])
    norm_g = inp("norm_g", [24, 128])
    w1 = inp("w1", [2, D, FF])
    w3 = inp("w3", [2, D, FF])
    w2 = inp("w2", [2, FF, D])
    w_in = inp("w_in", [D, IN_COLS])
    w_out = inp("w_out", [D, D])
    final_g = inp("final_g", [D])
    gconv = inp("gconv", [60, 128])
    gvec = inp("gvec", [16])
    gnw = inp("gnw", [128])
    lconv = inp("lconv", [24, 128])
    lwg = inp("lwg", [2, 2, 8, 64, 64])
    lvec = inp("lvec", [24, 128])
    offm = inp("offm", [14, 128, 128])
    y = nc.dram_tensor("y", [OWN, D], F32, kind="ExternalOutput").ap()

    dbg = {}

    def dbg_out(name, shape, dt=F32):
        dbg[name] = nc.dram_tensor("dbg_" + name if not name.endswith("_d") else name, list(shape), dt, kind="ExternalOutput").ap()
        return dbg[name]

    def scratch(name, shape, dt):
        if debug and name in debug:
            return dbg_out(name, shape, dt)
        return nc.dram_tensor(name, list(shape), dt).ap()

    u2T_d = scratch("u2T_d", [KC, 128, TALL], BF16)
    h1_d = scratch("h1_d", [OWN, D], F32)
    pT_d = scratch("pT_d", [20, 128, TP], BF16)
    z_d = scratch("z_d", [OWN, 512], F32)
    ba_d = scratch("ba_d", [TALL, 16], F32)
    qkvT_d = scratch("qkvT_d", [12, 128, TALL], BF16)
    ymixT_d = scratch("ymixT_d", [KC, 128, OWN], BF16)
    h2_d = scratch("h2_d", [OWN, D], F32)

    with ExitStack() as es:
        P = Prog(nc, es)
        P.psum_pool(6, 2)
        ident = P.gsb("ident", [128, 128], F32)
        modcol = P.gsb("modcol", [128, NMOD * KC, 2], F32)
        gcol = P.gsb("gcol", [128, 24], F32)
        gscol = P.gsb("gscol", [128, 3, 2, KC], F32)
        gate_b = P.gsb("gate_b", [128, 4, D], F32)
        sel = P.gsb("sel", [2, 2, 128], F32)
        id2 = P.gsb("id2", [2, 2], F32)

        def ident_setup():
            P.memset(ident[:], 1.0, w=[ident])
            P.op("gpsimd", lambda e: e.affine_select(out=ident[:], in_=ident[:], pattern=[[-1, 128]],
                                                      compare_op=ALU.is_equal, fill=0.0, base=0, channel_multiplier=1),
                 r=[ident], w=[ident])
            P.memset(sel[:], 0.0, w=[sel])
            P.op("gpsimd", lambda e: e.memset(sel[0:1, 0, :], 1.0), r=[sel], w=[sel])
            P.op("gpsimd", lambda e: e.memset(sel[:, 1, :], 1.0), r=[sel], w=[sel])
            P.op("gpsimd", lambda e: e.affine_select(out=sel[:, 1, :], in_=sel[:, 1, :], pattern=[[0, 128]],
                                                      compare_op=ALU.is_equal, fill=0.0, base=-1, channel_multiplier=1),
                 r=[sel], w=[sel])
            P.cp(id2[:], ident[0:2, 0:2], r=[ident], w=[id2], eng="gpsimd")

        def load_ffn_weights(li, w1b, w3b, w2b):
            w1v = w1[li].rearrange("(kc p) n -> p kc n", p=128)
            w3v = w3[li].rearrange("(kc p) n -> p kc n", p=128)
            w2v = w2[li].rearrange("(fc p) n -> p fc n", p=128)
            hf = FF // 2
            for kc in range(KC):
                for h2 in range(2):
                    P.dma(w1b[:, kc, h2 * hf:(h2 + 1) * hf], w1v[:, kc, h2 * hf:(h2 + 1) * hf], r=[], w=[w1b], lane="ld_w0", eng="gpsimd")
                    P.dma(w3b[:, kc, h2 * hf:(h2 + 1) * hf], w3v[:, kc, h2 * hf:(h2 + 1) * hf], r=[], w=[w3b], lane="ld_w1", eng="gpsimd")
            for fc in range(FC):
                P.dma(w2b[:, fc, :], w2v[:, fc, :], r=[], w=[w2b], lane="ld_w2", eng="gpsimd")

        with ExitStack() as st:
            P.st = st
            ident_setup()
            ccs = P.sb("s0_cc", [16, 128], F32)
            scT = P.sb("s0_scT", [128, 16], F32)
            ngs = P.sb("s0_ng", [24, 128], F32)
            bada = P.sb("s0_bada", [1, NMOD * D], F32)
            ones2 = P.sb("s0_ones2", [1, 2], F32)
            modrow = P.sb("s0_modrow", [2, NMOD * D], F32)
            wbuf = [P.sb("s0_wb%d" % i, [128, KC, 512], F32) for i in range(2)]
            P.dma(ccs[:], cc, r=[], w=[ccs], lane="ld_a")
            P.dma(ngs[:], norm_g, r=[], w=[ngs], lane="ld_a")
            P.dma(bada[:], b_ada, r=[], w=[bada], lane="ld_a")
            P.memset(ones2[:], 1.0, w=[ones2])
            P.act(ccs[:], ccs[:], AF.Silu, r=[ccs], w=[ccs])
            ps = P.psum()
            P.tr(ps[:, 0:16], ccs[:], ident[0:16, 0:16], r=[ccs, ident], w=[ps])
            P.cp(scT[:], ps[:, 0:16], r=[ps], w=[scT])
            ps = P.psum()
            P.tr(ps[:, 0:24], ngs[:], ident[0:24, 0:24], r=[ngs, ident], w=[ps])
            P.cp(gcol[:], ps[:, 0:24], r=[ps], w=[gcol])
            wv = w_ada.rearrange("(kc p) n -> p kc n", p=128)
            NB = NMOD * D // 512
            for nb in range(NB):
                wb = wbuf[nb % 2]
                P.dma(wb[:], wv[:, :, nb * 512:(nb + 1) * 512], r=[], w=[wb], lane="ld_w%d" % (nb % 2))
                ps = P.psum()
                for kc in range(KC):
                    P.mm(ps[0:2, :], lhsT=scT[:, kc::8], rhs=wb[:, kc, :], start=(kc == 0), stop=False, r=[scT, wb], w=[ps])
                P.mm(ps[0:2, :], lhsT=ones2[:], rhs=bada[:, nb * 512:(nb + 1) * 512], start=False, stop=True, r=[ones2, bada], w=[ps])
                P.cp(modrow[:, nb * 512:(nb + 1) * 512], ps[0:2, :], r=[ps], w=[modrow], eng="scalar")
            ps = P.psum()
            for j in range(NMOD * KC):
                P.mm(ps[:, 2 * j:2 * j + 2], lhsT=modrow[:, j * 128:(j + 1) * 128], rhs=id2[:], start=True, stop=True,
                     r=[modrow, id2], w=[ps])
            P.cp(modcol[:].rearrange("p j s -> p (j s)"), ps[:, 0:2 * NMOD * KC], r=[ps], w=[modcol])
            for i in range(3):
                for s in range(2):
                    P.stt(gscol[:, i, s, :], in0=modcol[:, (3 * i + 1) * KC:(3 * i + 2) * KC, s], scalar=1.0,
                          in1=gcol[:, i * KC:(i + 1) * KC], op0=ALU.add, op1=ALU.mult, r=[modcol, gcol], w=[gscol])
            for gi, (mod, s, sc) in enumerate([(2, 0, 0.5), (2, 1, 0.5), (5, 0, 1.0), (8, 0, 0.5)]):
                for hh in range(2):
                    ps = P.psum()
                    P.mm(ps[:], lhsT=sel[:, s, :], rhs=modrow[:, mod * D + hh * 512: mod * D + (hh + 1) * 512],
                         start=True, stop=True, r=[sel, modrow], w=[ps])
                    P.act(gate_b[:, gi, hh * 512:(hh + 1) * 512], ps[:], AF.Identity, scale=sc, r=[ps], w=[gate_b])
            if debug and "modrow" in debug:
                d = dbg_out("modrow", [2, NMOD * D])
                P.dma(d, modrow[:], r=[modrow], w=["dbg_modrow"], lane="st_a")
                d = dbg_out("gate_b", [128, 4, D])
                P.dma(d, gate_b[:], r=[gate_b], w=["dbg_gate_b"], lane="st_a")
                d = dbg_out("gscol", [128, 3, 2, KC])
                P.dma(d, gscol[:], r=[gscol], w=["dbg_gscol"], lane="st_a")
            P.finish()
            P.emit()

        if stop_after == "S0":
            return nc, I, dbg

        def norm_gen(xs, nsub, i, s, xn, uT, ssq, rstd, dst=None, dkey=None):
            if dst is None:
                dst = lambda kc: uT[:, kc, 0:nsub * 128]
                dkey = lambda kc: uT.name
            for j in range(nsub):
                P.act(xn[j][:], xs[j][:], AF.Square, accum_out=ssq[:, j:j + 1], r=[xs[j]], w=[xn[j], ssq])
            P.act(rstd[:, 0:nsub], ssq[:, 0:nsub], AF.Ln, scale=1.0 / D, bias=EPS, r=[ssq], w=[rstd])
            P.act(rstd[:, 0:nsub], rstd[:, 0:nsub], AF.Exp, scale=-0.5, r=[rstd], w=[rstd])
            for j in range(nsub):
                P.ts(xn[j][:], xs[j][:], rstd[:, j:j + 1], ALU.mult, r=[xs[j], rstd], w=[xn[j]])
            yield
            for kc in range(KC):
                ps = P.psum()
                for j in range(nsub):
                    P.tr(ps[:, j * 128:(j + 1) * 128], xn[j][:, kc * 128:(kc + 1) * 128], ident[:], r=[xn[j], ident], w=[ps])
                P.act(dst(kc), ps[:, 0:nsub * 128], AF.Identity,
                      scale=gscol[:, i, s, kc:kc + 1], bias=modcol[:, (3 * i) * KC + kc, s:s + 1],
                      r=[ps, gscol, modcol], w=[dkey(kc)])

        def norm_mod_T(xs, nsub, i, s, xn, uT, ssq, rstd):
            for _ in norm_gen(xs, nsub, i, s, xn, uT, ssq, rstd):
                pass

        def ffn_core(xs, nsub, uT, hT, w1b, w3b, w2b, gate_idx, sg, tmp, hook=None, hook0=None):
            n = nsub * 128
            for fc in range(FC):
                if hook0 is not None and fc == 2:
                    for _ in hook0:
                        pass
                if hook is not None and fc == FC - 2:
                    next(hook, None)
                p1 = P.psum()
                p3 = P.psum()
                for kc in range(KC):
                    P.mm(p1[:, 0:n], lhsT=w1b[:, kc, fc * 128:(fc + 1) * 128], rhs=uT[:, kc, 0:n], start=(kc == 0), stop=(kc == KC - 1),
                         r=[w1b, uT], w=[p1])
                for kc in range(KC):
                    P.mm(p3[:, 0:n], lhsT=w3b[:, kc, fc * 128:(fc + 1) * 128], rhs=uT[:, kc, 0:n], start=(kc == 0), stop=(kc == KC - 1),
                         r=[w3b, uT], w=[p3])
                s_ = sg[fc % 2]
                P.act(s_[:, 0:n], p1[:, 0:n], AF.Silu, r=[p1], w=[s_])
                P.tt(hT[:, fc, 0:n], s_[:, 0:n], p3[:, 0:n], ALU.mult, r=[s_, p3], w=[(hT.name, fc)])
            for j in range(nsub):
                for hh in range(2):
                    po = P.psum()
                    for fc in range(FC):
                        P.mm(po[:], lhsT=hT[:, fc, j * 128:(j + 1) * 128], rhs=w2b[:, fc, hh * 512:(hh + 1) * 512],
                             start=(fc == 0), stop=(fc == FC - 1), r=[(hT.name, fc), w2b], w=[po])
                    P.tt(tmp[:], po[:], gate_b[:, gate_idx, hh * 512:(hh + 1) * 512], ALU.mult, r=[po, gate_b], w=[tmp])
                    P.tt(xs[j][:, hh * 512:(hh + 1) * 512], xs[j][:, hh * 512:(hh + 1) * 512], tmp[:], ALU.add, r=[xs[j], tmp], w=[xs[j]])
                    if hook is not None and j == 0 and hh == 0:
                        next(hook, None)

        with ExitStack() as st:
            P.st = st
            w1b = P.sb("s1_w1b", [128, KC, FF], BF16)
            w3b = P.sb("s1_w3b", [128, KC, FF], BF16)
            w2b = P.sb("s1_w2b", [128, FC, D], BF16)
            load_ffn_weights(0, w1b, w3b, w2b)
            nsub = NT // 128
            xss = [[P.sb("s1_x%d_%d" % (q_, j), [128, D], F32) for j in range(nsub)] for q_ in range(2)]
            xn = [P.sb("s1_xn%d" % j, [128, D], F32) for j in range(nsub)]
            uT = P.sb("s1_uT", [128, KC, NT], BF16)
            hT = P.sb("s1_hT", [128, FC, NT], BF16)
            sg = [P.sb("s1_sg%d" % j, [128, NT], F32) for j in range(2)]
            tmp = P.sb("s1_tmp", [128, 512], F32)
            ssq = P.sb("s1_ssq", [128, 4], F32)
            rstd = P.sb("s1_rstd", [128, 4], F32)
            ntiles = TALL // NT
            if stop_after == "S1a":
                ntiles = 3
            def s1_load(ti_):
                t0_ = ti_ * NT
                for j in range(nsub):
                    src = ctx[t0_ + j * 128: t0_ + (j + 1) * 128, :] if t0_ < CTX else x[t0_ - CTX + j * 128: t0_ - CTX + (j + 1) * 128, :]
                    P.dma(xss[ti_ % 2][j][:], src, r=[], w=[xss[ti_ % 2][j]], lane="ld_x%d_%d" % (ti_ % 2, j))
            s1_load(0)
            s_of = lambda ti_: 1 if ti_ * NT < CTX else 0
            ssq2 = P.sb("s1_ssq2", [128, 4], F32)
            rstd2 = P.sb("s1_rstd2", [128, 4], F32)
            norm_mod_T(xss[0], nsub, 0, s_of(0), xn, uT, ssq, rstd)
            pending_n2 = None
            for ti in range(ntiles):
                t0 = ti * NT
                is_ctx = t0 < CTX
                s = 1 if is_ctx else 0
                xs = xss[ti % 2]
                hook = None
                if ti + 1 < ntiles:
                    s1_load(ti + 1)
                    hook = norm_gen(xss[(ti + 1) % 2], nsub, 0, s_of(ti + 1), xn, uT, ssq, rstd)
                ffn_core(xs, nsub, uT, hT, w1b, w3b, w2b, 1 if is_ctx else 0, sg, tmp, hook=hook, hook0=pending_n2)
                pending_n2 = None
                if hook is not None:
                    for _ in hook:
                        pass
                lat0 = t0 - CTX
                if (not is_ctx) and lat0 < OWN:
                    for j in range(nsub):
                        P.dma(h1_d[lat0 + j * 128: lat0 + (j + 1) * 128, :], xs[j][:], r=[xs[j]], w=["h1_d"], lane="st_a%d_%d" % (ti % 2, j))
                g2 = norm_gen(xs, nsub, 1, s, xn, uT, ssq2, rstd2,
                              dst=lambda kc: hT[:, FC - KC + kc, 0:nsub * 128], dkey=lambda kc: (hT.name, FC - KC + kc))
                next(g2)

                def n2_tail(g_, t0_):
                    for _ in g_:
                        pass
                    P.dma(u2T_d[:, :, t0_:t0_ + NT].rearrange("k p t -> p k t"), hT[:, FC - KC:FC, :],
                          r=[(hT.name, FC - KC + kc) for kc in range(KC)], w=["u2T_d"], lane="st_b")
                    yield
                pending_n2 = n2_tail(g2, t0)
            for _ in pending_n2:
                pass
            P.finish()
            P.emit()
        if stop_after in ("S1", "S1a"):
            return nc, I, dbg

        FM_COLS = [c * 128 for c in range(12)] + [2064 + c * 128 for c in range(8)]
        with ExitStack() as st:
            P.st = st
            winb = P.sb("s2_winb", [128, KC, IN_COLS], BF16)
            wiv = w_in.rearrange("(kc p) n -> p kc n", p=128)
            hc = IN_COLS // 2
            for kc in range(KC):
                for h2 in range(2):
                    P.dma(winb[:, kc, h2 * hc:(h2 + 1) * hc], wiv[:, kc, h2 * hc:(h2 + 1) * hc], r=[], w=[winb], lane="ld_w0", eng="gpsimd")
            u2 = [P.sb("s2_u%d" % i, [128, KC, 512], BF16) for i in range(2)]
            pst = [P.sb("s2_pst%d" % i, [128, 20, 512], BF16) for i in range(2)]
            zst = [P.sb("s2_z%d" % i, [128, 512], F32) for i in range(2)]
            bast = [P.sb("s2_ba%d" % i, [128, 16], F32) for i in range(2)]
            zer = P.sb("s2_zer", [128, 20, 4], BF16)
            P.memset(zer[:], 0.0, w=[zer])
            pv = pT_d.rearrange("c p t -> p c t")
            P.dma(pv[:, :, 0:2], zer[:, :, 0:2], r=[zer], w=["pT_d"], lane="st_c")
            P.dma(pv[:, :, 2 + CTX:6 + CTX], zer[:, :, 0:4], r=[zer], w=["pT_d"], lane="st_c")
            P.dma(pv[:, :, TP - 2:TP], zer[:, :, 0:2], r=[zer], w=["pT_d"], lane="st_c")
            tiles2 = [(0, CTX)] + [(CTX + i * 512, 512) for i in range(SEQ // 512)]
            if stop_after == "S2a":
                tiles2 = tiles2[:2]
            zi = 0
            for ti, (t0, n) in enumerate(tiles2):
                ub = u2[ti % 2]
                pb = pst[ti % 2]
                P.dma(ub[:, :, 0:n], u2T_d[:, :, t0:t0 + n].rearrange("k p t -> p k t"), r=["u2T_d"], w=[ub], lane="ld_x%d" % (ti % 2))
                own_tile = (t0 >= CTX and t0 - CTX < OWN)
                for ci, c0 in enumerate(FM_COLS):
                    halo_tile = (t0 >= CTX and t0 - CTX < OWN + 512)
                    if (ci >= 16 and not own_tile) or (ci < 4 and not halo_tile):
                        continue
                    ps = P.psum()
                    for kc in range(KC):
                        P.mm(ps[:, 0:n], lhsT=winb[:, kc, c0:c0 + 128], rhs=ub[:, kc, 0:n], start=(kc == 0), stop=(kc == KC - 1), r=[winb, ub], w=[ps])
                    if ci % 2 == 0:
                        P.cp(pb[:, ci, 0:n], ps[:, 0:n], r=[ps], w=[pb], eng="scalar")
                    else:
                        P.cp(pb[:, ci, 0:n], ps[:, 0:n], r=[ps], w=[pb], eng="vector")
                P.dma(pv[:, :, pcol(t0):pcol(t0) + n], pb[:, :, 0:n], r=[pb], w=["pT_d"], lane="st_a%d" % (ti % 2))
                for j in range(n // 128):
                    tok = t0 + j * 128
                    zb = zst[zi % 2]
                    bb = bast[zi % 2]
                    if tok >= CTX and tok - CTX < OWN:
                        ps = P.psum()
                        for kc in range(KC):
                            P.mm(ps[:], lhsT=ub[:, kc, j * 128:(j + 1) * 128], rhs=winb[:, kc, 1536:2048], start=(kc == 0), stop=(kc == KC - 1), r=[winb, ub], w=[ps])
                        P.act(zb[:], ps[:], AF.Silu, r=[ps], w=[zb])
                        P.dma(z_d[tok - CTX:tok - CTX + 128, :], zb[:], r=[zb], w=["z_d"], lane="st_b%d" % (zi % 2))
                    ps = P.psum()
                    for kc in range(KC):
                        P.mm(ps[:, 0:16], lhsT=ub[:, kc, j * 128:(j + 1) * 128], rhs=winb[:, kc, 2048:2064], start=(kc == 0), stop=(kc == KC - 1), r=[winb, ub], w=[ps])
                    P.cp(bb[:], ps[:, 0:16], r=[ps], w=[bb])
                    P.dma(ba_d[tok:tok + 128, :], bb[:], r=[bb], w=["ba_d"], lane="st_d%d" % (zi % 2))
                    zi += 1
            if debug and "winb" in debug:
                d = dbg_out("winb", [128, KC, IN_COLS], BF16)
                P.dma(d, winb[:], r=[winb], w=["dbg_winb"], lane="st_c")
                d = dbg_out("u2", [128, KC, 512], BF16)
                P.dma(d, u2[1][:], r=[u2[1]], w=["dbg_u2"], lane="st_c")
            P.finish()
            P.emit()
        if stop_after in ("S2", "S2a"):
            return nc, I, dbg

        with ExitStack() as st:
            P.st = st
            cwr = P.sb("s3p_cwr", [60, 128], F32)
            cw = P.sb("s3p_cw", [128, 60], F32)
            convd = P.sb("s3p_convd", [128, 60, 128], BF16)
            ones = P.sb("s3p_ones", [128, 128], BF16)
            raw = [P.sb("s3p_raw%d" % i, [128, 12, 516], BF16) for i in range(2)]
            sv = [P.sb("s3p_s%d" % c, [128, 512], F32) for c in range(8)]
            sq = [P.sb("s3p_sq%d" % i, [128, 512], BF16) for i in range(2)]
            rn = [P.sb("s3p_rn%d" % i, [128, 512], F32) for i in range(2)]
            ost = [P.sb("s3p_o%d" % i, [128, 12, 512], BF16) for i in range(2)]
            P.dma(cwr[:], gconv, r=[], w=[cwr], lane="ld_a")
            P.memset(ones[:], 1.0, w=[ones])
            ps = P.psum()
            P.tr(ps[:, 0:60], cwr[:], ident[0:60, 0:60], r=[cwr, ident], w=[ps])
            P.cp(cw[:], ps[:, 0:60], r=[ps], w=[cw])
            for i in range(60):
                P.ts(convd[:, i, :], ident[:], cw[:, i:i + 1], ALU.mult, r=[ident, cw], w=[convd], eng="gpsimd")
            qv = qkvT_d.rearrange("c p t -> p c t")
            tiles3 = [(0, CTX)] + [(CTX + i * 512, 512) for i in range(SEQ // 512)]
            if stop_after == "S3pa":
                tiles3 = tiles3[:2]
            for ti, (t0, n) in enumerate(tiles3):
                rb = raw[ti % 2]
                ob = ost[ti % 2]
                P.dma(rb[:, :, 0:n + 4], pv[:, 0:12, pcol(t0) - 2:pcol(t0) + n + 2], r=["pT_d"], w=[rb], lane="ld_x%d" % (ti % 2))
                own_tile = (t0 >= CTX and t0 - CTX < OWN)
                c_lo = 0 if own_tile else 4
                for c in range(c_lo, 12):
                    ps = P.psum()
                    for k in range(5):
                        P.mm(ps[:, 0:n], lhsT=convd[:, c * 5 + k, :], rhs=rb[:, c, k:k + n], start=(k == 0), stop=(k == 4), r=[convd, rb], w=[ps])
                    if c < 8:
                        P.act(sv[c][:, 0:n], ps[:, 0:n], AF.Silu, r=[ps], w=[sv[c]])
                    else:
                        P.act(ob[:, c, 0:n], ps[:, 0:n], AF.Silu, r=[ps], w=[ob])
                for c in range(c_lo, 8):
                    q_ = sq[c % 2]
                    r_ = rn[c % 2]
                    P.tt(q_[:, 0:n], sv[c][:, 0:n], sv[c][:, 0:n], ALU.mult, r=[sv[c]], w=[q_], eng="gpsimd")
                    ps = P.psum()
                    P.mm(ps[:, 0:n], lhsT=ones[:], rhs=q_[:, 0:n], start=True, stop=True, r=[ones, q_], w=[ps])
                    P.act(r_[:, 0:n], ps[:, 0:n], AF.Ln, bias=EPS, r=[ps], w=[r_])
                    if c < 4:
                        P.act(r_[:, 0:n], r_[:, 0:n], AF.Exp, scale=-0.5, bias=float(np.log(128.0 ** -0.5)), r=[r_], w=[r_])
                    else:
                        P.act(r_[:, 0:n], r_[:, 0:n], AF.Exp, scale=-0.5, r=[r_], w=[r_])
                    P.tt(ob[:, c, 0:n], sv[c][:, 0:n], r_[:, 0:n], ALU.mult, r=[sv[c], r_], w=[ob])
                P.dma(qv[:, :, t0:t0 + n], ob[:, :, 0:n], r=[ob], w=["qkvT_d"], lane="st_a%d" % (ti % 2))
            P.finish()
            P.emit()
        if stop_after in ("S3p", "S3pa"):
            return nc, I, dbg

        NTL = TALL // 128
        NOUT = OWN // 128
        with ExitStack() as st:
            P.st = st
            base = {}
            for nm, pat, cm, cmpop in [("UPi", 1, -1, ALU.is_ge), ("LOs", -1, 1, ALU.is_gt), ("LOi", -1, 1, ALU.is_ge), ("UPs", 1, -1, ALU.is_gt)]:
                t = P.sb("s3_" + nm, [128, 128], F32)
                P.memset(t[:], 1.0, w=[t])
                P.op("gpsimd", lambda e, t=t, pat=pat, cm=cm, cmpop=cmpop: e.affine_select(
                    out=t[:], in_=t[:], pattern=[[pat, 128]], compare_op=cmpop, fill=0.0, base=0, channel_multiplier=cm), r=[t], w=[t])
                base[nm] = t
            identb = P.sb("s3_identb", [128, 128], BF16)
            P.cp(identb[:], ident[:], r=[ident], w=[identb], eng="gpsimd")
            ones = P.sb("s3_ones", [128, 128], F32)
            P.memset(ones[:], 1.0, w=[ones])
            OFFs = P.sb("s3_offs", [128, 14, 128], BF16)
            for mi in range(14):
                P.dma(OFFs[:, mi, :], offm[mi], r=[], w=[OFFs], lane="ld_w0", eng="gpsimd")
            nw4 = P.sb("s3_nw4", [128, 4, 128], F32)
            P.dma(nw4[:], bass.AP(gnw.tensor, 0, [[0, 128], [0, 4], [1, 128]]), r=[], w=[nw4], lane="ld_a")
            ba = P.sb("s3_ba", [128, NTL, 16], F32)
            gv = P.sb("s3_gv", [128, NTL, 16], F32)
            P.dma(ba[:], ba_d.rearrange("(n p) c -> p n c", p=128), r=["ba_d"], w=[ba], lane="ld_b")
            P.dma(gv[:], bass.AP(gvec.tensor, 0, [[0, 128], [0, NTL], [1, 16]]), r=[], w=[gv], lane="ld_c")
            if stop_after == "S3g2":
                P.finish()
                P.emit()
                return nc, I, dbg
            GA = {nm: P.sb("s3_g_" + nm, [128, 2, NTL, 4], F32) for nm in
                  ["beta", "g", "gcum", "egc", "egt", "ekd", "neb", "t1", "t2", "t3"]}
            for d in range(2):
                bsl = ba[:, :, d * 4:(d + 1) * 4]
                asl = ba[:, :, 8 + d * 4:8 + (d + 1) * 4]
                t1, t2, t3 = GA["t1"][:, d], GA["t2"][:, d], GA["t3"][:, d]
                P.act(t1, bsl, AF.Exp, scale=-1.0, r=[ba], w=[GA["t1"]])
                P.ts(t1, t1, 1.0, ALU.add, r=[GA["t1"]], w=[GA["t1"]])
                P.op("vector", lambda e, o=GA["beta"][:, d], i=t1: e.reciprocal(out=o, in_=i), r=[GA["t1"]], w=[GA["beta"]])
                P.tt(t2, asl, gv[:, :, 8 + d * 4:8 + (d + 1) * 4], ALU.add, r=[ba, gv], w=[GA["t2"]])
                P.act(t3, t2, AF.Abs, r=[GA["t2"]], w=[GA["t3"]])
                P.act(t3, t3, AF.Exp, scale=-1.0, r=[GA["t3"]], w=[GA["t3"]])
                P.act(t3, t3, AF.Ln, bias=1.0, r=[GA["t3"]], w=[GA["t3"]])
                P.stt(t2, in0=t2, scalar=0.0, in1=t3, op0=ALU.max, op1=ALU.add, r=[GA["t2"], GA["t3"]], w=[GA["t2"]])
                P.act(t3, gv[:, :, d * 4:(d + 1) * 4], AF.Exp, r=[gv], w=[GA["t3"]])
                P.stt(GA["g"][:, d], in0=t2, scalar=-1.0, in1=t3, op0=ALU.mult, op1=ALU.mult, r=[GA["t2"], GA["t3"]], w=[GA["g"]])
            if stop_after == "S3g3":
                P.finish()
                P.emit()
                return nc, I, dbg
            psF = P.psum("a")
            gflat = lambda a, d: a[:, d].rearrange("p n h -> p (n h)")
            P.mm(psF[:, 0:NTL * 4], lhsT=base["UPi"][:], rhs=gflat(GA["g"], 0), start=True, stop=True, r=[base["UPi"], GA["g"]], w=[psF])
            P.mm(psF[:, NTL * 4:NTL * 8], lhsT=base["LOi"][:], rhs=gflat(GA["g"], 1), start=True, stop=True, r=[base["LOi"], GA["g"]], w=[psF])
            psT = P.psum("a")
            P.mm(psT[:, 0:NTL * 8], lhsT=ones[:], rhs=GA["g"][:].rearrange("p d n h -> p (d n h)"), start=True, stop=True, r=[ones, GA["g"]], w=[psT])
            if stop_after == "S3g4":
                P.finish()
                P.emit()
                return nc, I, dbg
            fl = lambda a: a[:].rearrange("p d n h -> p (d n h)")
            P.cp(fl(GA["gcum"]), psF[:, 0:NTL * 8], r=[psF], w=[GA["gcum"]])
            P.act(fl(GA["egc"]), fl(GA["gcum"]), AF.Exp, r=[GA["gcum"]], w=[GA["egc"]])
            P.act(fl(GA["egt"]), psT[:, 0:NTL * 8], AF.Exp, r=[psT], w=[GA["egt"]])
            if stop_after == "S3g5":
                P.finish()
                P.emit()
                return nc, I, dbg
            P.act(fl(GA["t2"]), psT[:, 0:NTL * 8], AF.Identity, r=[psT], w=[GA["t2"]])
            P.tt(fl(GA["t1"]), fl(GA["t2"]), fl(GA["gcum"]), ALU.subtract, r=[GA["t2"], GA["gcum"]], w=[GA["t1"]])
            if stop_after == "S3g7":
                P.finish()
                P.emit()
                return nc, I, dbg
            P.act(fl(GA["ekd"]), fl(GA["t1"]), AF.Exp, r=[GA["t1"]], w=[GA["ekd"]])
            if stop_after == "S3g8":
                P.finish()
                P.emit()
                return nc, I, dbg
            P.stt(fl(GA["neb"]), in0=fl(GA["beta"]), scalar=-1.0, in1=fl(GA["egc"]), op0=ALU.mult, op1=ALU.mult, r=[GA["beta"], GA["egc"]], w=[GA["neb"]])

            if stop_after == "S3g6":
                P.finish()
                P.emit()
                return nc, I, dbg
            if stop_after == "S3g":
                for nm in ["beta", "g", "gcum", "egt", "ekd", "neb"]:
                    dd = dbg_out("g_" + nm, [128, 2, NTL, 4])
                    P.dma(dd, GA[nm][:], r=[GA[nm]], w=["dbg_g_" + nm], lane="st_c")
                P.finish()
                P.emit()
                return nc, I, dbg
            o_acc = P.sb("s3_oacc", [128, NOUT, 4, 128], F32)
            okeys = [("o_acc", lt, h) for lt in range(NOUT) for h in range(4)]
            P.memset(o_acc[:], 0.0, w=okeys)
            ld = [P.sb("s3_ld%d" % i, [128, 12, 128], BF16) for i in range(4)]
            zb = [P.sb("s3_zb%d" % i, [128, 4, 128], F32) for i in range(2)]
            SL = {}
            for nm, dt_ in [("kdec", BF16), ("kb", BF16), ("vb", F32), ("KbT", BF16), ("Ag", F32), ("e", F32), ("eT", F32),
                            ("dec_s", F32), ("decT_s", F32), ("decT_i", F32), ("attnT", BF16), ("XT", BF16)]:
                nsl = 3 if nm in ("kdec", "vb", "attnT", "XT") else 2
                SL[nm] = [P.sb("s3_%s%d" % (nm, i), [128, 4, 128], dt_) for i in range(nsl)]
            Tb = [[P.sb("s3_T%d_%d" % (p_, i), [128, 4, 128], BF16) for i in range(2)] for p_ in range(2)]
            TTb = [[P.sb("s3_TT%d_%d" % (p_, i), [128, 4, 128], BF16) for i in range(2)] for p_ in range(2)]
            NLb = [P.sb("s3_NL%d" % p_, [128, 4, 128], BF16) for p_ in range(2)]
            NLTb = [P.sb("s3_NLT%d" % p_, [128, 4, 128], BF16) for p_ in range(2)]
            NoA = [[P.sb("s3_NoA%d_%d" % (p_, i), [128, 4, 128], BF16) for i in range(2)] for p_ in range(2)]
            NoB = [[P.sb("s3_NoB%d_%d" % (p_, i), [128, 4, 128], BF16) for i in range(2)] for p_ in range(2)]
            Yb = [P.sb("s3_Y%d" % p_, [128, 4, 128], BF16) for p_ in range(2)]
            YT = [P.sb("s3_YT%d" % p_, [128, 4, 128], BF16) for p_ in range(2)]
            S = [P.sb("s3_S%d" % h, [128, 128], F32) for h in range(4)]
            Sb = [P.sb("s3_Sb%d" % h, [128, 128], BF16) for h in range(4)]
            rhs2 = [P.sb("s3_rhs2_%d" % h, [128, 128], BF16) for h in range(4)]
            vn = [P.sb("s3_vn%d" % h, [128, 128], BF16) for h in range(4)]
            sqo = P.sb("s3_sqo", [128, 4, 128], F32)
            ms4 = P.sb("s3_ms4", [128, 4], F32)
            t1o = P.sb("s3_t1o", [128, 4, 128], F32)
            nwz = P.sb("s3_nwz", [128, 4, 128], F32)
            yb = P.sb("s3_yb", [128, 4, 128], BF16)
            yT = [P.sb("s3_yT%d" % i, [128, 4, 128], BF16) for i in range(2)]
            qv = qkvT_d.rearrange("c p t -> p c t")
            ymv = ymixT_d.rearrange("c p t -> p c t")

            def v4(ps, off=0):
                return ps[:, off:off + 512].rearrange("p (h i) -> p h i", h=4)

            def gb(nm, d, ti):
                return GA[nm][:, d, ti, :].unsqueeze(2).broadcast_to([128, 4, 128])

            def load(ti, sl3):
                P.dma(ld[sl3][:], qv[:, :, ti * 128:(ti + 1) * 128], r=["qkvT_d"], w=[ld[sl3]], lane="ld_q%d" % sl3)

            def bh(ap2d):
                return ap2d.unsqueeze(1).broadcast_to([128, 4, 128])

            def prep(ti, d, sl, sl3, need_out, pset):
                qk = ld[sl3]
                A4 = bh(base["UPi"][:]) if d == 0 else bh(base["LOi"][:])
                Bm = base["LOs"] if d == 0 else base["UPs"]
                mL = bh(base["LOs"][:]) if d == 0 else bh(base["UPs"][:])
                mLT = bh(base["UPs"][:]) if d == 0 else bh(base["LOs"][:])
                mAT = bh(base["UPi"][:]) if d == 0 else bh(base["LOi"][:])
                I4 = bh(identb[:])
                kdec, vb, attnT, XTf = [SL[k][sl] for k in ["kdec", "vb", "attnT", "XT"]]
                kb, KbT, Ag, e_, eT, dec_s, decT_s, decT_i = [SL[k][pset] for k in
                    ["kb", "KbT", "Ag", "e", "eT", "dec_s", "decT_s", "decT_i"]]
                NL, NLT = NLb[pset], NLTb[pset]
                NoA_, NoB_ = NoA[pset], NoB[pset]
                Tb_, TTb_ = Tb[pset], TTb[pset]
                YT_, Yb_ = YT[pset], Yb[pset]
                offA = lambda li: bh(OFFs[:, 2 * li + (0 if d == 0 else 1), :])
                offB = lambda li: bh(OFFs[:, 2 * li + (1 if d == 0 else 0), :])
                pb = P.psumb()
                for h in range(4):
                    P.tr(pb[:, h * 128:(h + 1) * 128], qk[:, 4 + h, :], identb[:], r=[qk, identb], w=[pb])
                for h in range(4):
                    P.tr(pb[:, 512 + h * 128:512 + (h + 1) * 128], qk[:, 8 + h, :], identb[:], r=[qk, identb], w=[pb])
                Abase = base["UPi"] if d == 0 else base["LOi"]
                for h in range(4):
                    P.act(Ag[:, h, :], Abase[:], AF.Copy, scale=GA["g"][:, d, ti, h:h + 1], r=[Abase, GA["g"]], w=[Ag])
                P.tt(kb[:], v4(pb), gb("beta", d, ti), ALU.mult, r=[pb, GA["beta"]], w=[kb])
                for h in range(4):
                    P.act(kdec[:, h, :], pb[:, h * 128:(h + 1) * 128], AF.Copy, scale=GA["ekd"][:, d, ti, h:h + 1], r=[pb, GA["ekd"]], w=[kdec])
                for h in range(4):
                    P.act(vb[:, h, :], pb[:, 512 + h * 128:512 + (h + 1) * 128], AF.Copy, scale=GA["beta"][:, d, ti, h:h + 1], r=[pb, GA["beta"]], w=[vb])
                yield
                psD = P.psum("a")
                psDT = P.psum("a")
                for h in range(4):
                    P.mm(psD[:, h * 128:(h + 1) * 128], lhsT=Ag[:, h, :], rhs=Bm[:], start=True, stop=True, r=[Ag, Bm], w=[psD])
                for h in range(4):
                    P.mm(psDT[:, h * 128:(h + 1) * 128], lhsT=Bm[:], rhs=Ag[:, h, :], start=True, stop=True, r=[Ag, Bm], w=[psDT])
                pb2 = P.psumb()
                for h in range(4):
                    P.tr(pb2[:, h * 128:(h + 1) * 128], kb[:, h, :], identb[:], r=[kb, identb], w=[pb2])
                P.act(e_[:], v4(psD), AF.Exp, r=[psD], w=[e_])
                P.act(eT[:], v4(psDT), AF.Exp, r=[psDT], w=[eT])
                P.cp(KbT[:], v4(pb2), r=[pb2], w=[KbT], eng="scalar")
                yield
                P.tt(dec_s[:], e_[:], mL, ALU.mult, r=[e_, base["LOs"], base["UPs"]], w=[dec_s], eng="gpsimd")
                P.tt(decT_s[:], eT[:], mLT, ALU.mult, r=[eT, base["LOs"], base["UPs"]], w=[decT_s], eng="gpsimd")
                if need_out:
                    P.tt(decT_i[:], eT[:], mAT, ALU.mult, r=[eT, base["UPi"], base["LOi"]], w=[decT_i], eng="gpsimd")
                yield
                pk = P.psum("a")
                pkT = P.psum("a")
                for h in range(4):
                    P.mm(pk[:, h * 128:(h + 1) * 128], lhsT=KbT[:, h, :], rhs=qk[:, 4 + h, :], start=True, stop=True, r=[KbT, qk], w=[pk])
                for h in range(4):
                    P.mm(pkT[:, h * 128:(h + 1) * 128], lhsT=qk[:, 4 + h, :], rhs=KbT[:, h, :], start=True, stop=True, r=[KbT, qk], w=[pkT])
                if need_out:
                    pq = P.psum("a")
                    for h in range(4):
                        P.mm(pq[:, h * 128:(h + 1) * 128], lhsT=qk[:, 4 + h, :], rhs=qk[:, h, :], start=True, stop=True, r=[qk], w=[pq])
                P.stt(NL[:], in0=v4(pk), scalar=-1.0, in1=dec_s[:], op0=ALU.mult, op1=ALU.mult, r=[pk, dec_s], w=[NL])
                P.stt(NLT[:], in0=v4(pkT), scalar=-1.0, in1=decT_s[:], op0=ALU.mult, op1=ALU.mult, r=[pkT, decT_s], w=[NLT])
                if need_out:
                    P.tt(attnT[:], v4(pq), decT_i[:], ALU.mult, r=[pq, decT_i], w=[attnT])
                yield
                T, TT = Tb_[0], TTb_[0]
                P.tt(NoA_[0][:], NL[:], offA(0), ALU.mult, r=[NL, OFFs], w=[NoA_[0]], eng="gpsimd")
                P.tt(NoB_[0][:], NLT[:], offB(0), ALU.mult, r=[NLT, OFFs], w=[NoB_[0]], eng="vector")
                P.tt(T[:], NoA_[0][:], I4, ALU.add, r=[NoA_[0], identb], w=[T], eng="gpsimd")
                P.tt(TT[:], NoB_[0][:], I4, ALU.add, r=[NoB_[0], identb], w=[TT], eng="vector")
                P.tt(NoA_[1][:], NL[:], offA(1), ALU.mult, r=[NL, OFFs], w=[NoA_[1]], eng="gpsimd")
                P.tt(NoB_[1][:], NLT[:], offB(1), ALU.mult, r=[NLT, OFFs], w=[NoB_[1]], eng="gpsimd")
                yield
                for li in range(1, 7):
                    last = (li == 6)
                    Tn = Tb_[li % 2]
                    TTn = XTf if last else TTb_[li % 2]
                    nA, nB = NoA_[li % 2], NoB_[li % 2]
                    pY = P.psum("a")
                    for h in range(4):
                        P.mm(pY[:, h * 128:(h + 1) * 128], lhsT=nA[:, h, :], rhs=TT[:, h, :], start=True, stop=True, r=[nA, TT], w=[pY])
                    if not last:
                        pY2 = P.psum("a")
                        for h in range(4):
                            P.mm(pY2[:, h * 128:(h + 1) * 128], lhsT=nB[:, h, :], rhs=T[:, h, :], start=True, stop=True, r=[nB, T], w=[pY2])
                    P.cp(YT_[:], v4(pY), r=[pY], w=[YT_], eng="scalar")
                    if not last:
                        P.cp(Yb_[:], v4(pY2), r=[pY2], w=[Yb_], eng=("scalar" if li % 2 == 0 else "vector"))
                        nA2, nB2 = NoA_[(li + 1) % 2], NoB_[(li + 1) % 2]
                        P.tt(nA2[:], NL[:], offA(li + 1), ALU.mult, r=[NL, OFFs], w=[nA2], eng="gpsimd")
                        if li + 1 < 6:
                            P.tt(nB2[:], NLT[:], offB(li + 1), ALU.mult, r=[NLT, OFFs], w=[nB2], eng=("gpsimd" if li % 2 else "vector"))
                    yield
                    pZ = P.psum("a")
                    for h in range(4):
                        P.mm(pZ[:, h * 128:(h + 1) * 128], lhsT=identb[:], rhs=TT[:, h, :], start=True, stop=False, r=[identb, TT], w=[pZ])
                        P.mm(pZ[:, h * 128:(h + 1) * 128], lhsT=T[:, h, :], rhs=YT_[:, h, :], start=False, stop=True, r=[T, YT_], w=[pZ])
                    if not last:
                        pZ2 = P.psum("a")
                        for h in range(4):
                            P.mm(pZ2[:, h * 128:(h + 1) * 128], lhsT=identb[:], rhs=T[:, h, :], start=True, stop=False, r=[identb, T], w=[pZ2])
                            P.mm(pZ2[:, h * 128:(h + 1) * 128], lhsT=TT[:, h, :], rhs=Yb_[:, h, :], start=False, stop=True, r=[TT, Yb_], w=[pZ2])
                    P.cp(TTn[:], v4(pZ), r=[pZ], w=[TTn], eng=("scalar" if li % 2 else "vector"))
                    if not last:
                        P.cp(Tn[:], v4(pZ2), r=[pZ2], w=[Tn], eng=("vector" if li % 2 else "scalar"))
                    T, TT = Tn, TTn
                    yield

            def scan(ti, d, sl, sl3, need_out):
                qk = ld[sl3]
                kdec, vb, attnT, XTf = SL["kdec"][sl], SL["vb"][sl], SL["attnT"][sl], SL["XT"][sl]
                lt = ti - 2
                gcol_ = lambda nm, h: GA[nm][:, d, ti, h:h + 1]
                pas, pvs, pds = {}, {}, {}
                for hp in range(2):
                    hs = [2 * hp, 2 * hp + 1]
                    pa = P.psum("b")
                    for j, h in enumerate(hs):
                        P.mm(pa[:, j * 128:(j + 1) * 128], lhsT=qk[:, 4 + h, :], rhs=Sb[h][:], start=True, stop=True, r=[qk, Sb[h]], w=[pa])
                    for j, h in enumerate(hs):
                        P.stt(rhs2[h][:], in0=pa[:, j * 128:(j + 1) * 128], scalar=gcol_("neb", h), in1=vb[:, h, :], op0=ALU.mult, op1=ALU.add,
                              r=[pa, GA["neb"], vb], w=[rhs2[h]])
                    yield
                    pv_ = P.psum("b")
                    for j, h in enumerate(hs):
                        P.mm(pv_[:, j * 128:(j + 1) * 128], lhsT=XTf[:, h, :], rhs=rhs2[h][:], start=True, stop=True, r=[XTf, rhs2[h]], w=[pv_])
                    for j, h in enumerate(hs):
                        P.cp(vn[h][:], pv_[:, j * 128:(j + 1) * 128], r=[pv_], w=[vn[h]], eng="scalar")
                    yield
                    if need_out:
                        pc = P.psum("b")
                        for j, h in enumerate(hs):
                            P.mm(pc[:, j * 256:j * 256 + 128], lhsT=qk[:, h, :], rhs=Sb[h][:], start=True, stop=True, r=[qk, Sb[h]], w=[pc])
                            P.mm(pc[:, j * 256 + 128:j * 256 + 256], lhsT=attnT[:, h, :], rhs=vn[h][:], start=True, stop=True, r=[attnT, vn[h]], w=[pc])
                        for j, h in enumerate(hs):
                            ok = ("o_acc", lt, h)
                            P.stt(o_acc[:, lt, h, :], in0=pc[:, j * 256:j * 256 + 128], scalar=gcol_("egc", h), in1=o_acc[:, lt, h, :], op0=ALU.mult, op1=ALU.add,
                                  r=[pc, GA["egc"], ok], w=[ok])
                            P.tt(o_acc[:, lt, h, :], o_acc[:, lt, h, :], pc[:, j * 256 + 128:j * 256 + 256], ALU.add, r=[pc, ok], w=[ok])
                        yield
                    pd = P.psum("b")
                    for j, h in enumerate(hs):
                        P.mm(pd[:, j * 128:(j + 1) * 128], lhsT=kdec[:, h, :], rhs=vn[h][:], start=True, stop=True, r=[kdec, vn[h]], w=[pd])
                    for j, h in enumerate(hs):
                        P.stt(S[h][:], in0=S[h][:], scalar=gcol_("egt", h), in1=pd[:, j * 128:(j + 1) * 128], op0=ALU.mult, op1=ALU.add, r=[S[h], GA["egt"], pd], w=[S[h]])
                        P.cp(Sb[h][:], S[h][:], r=[S[h]], w=[Sb[h]], eng="scalar")
                    yield

            def finalize(lt, zi):
                z_ = zb[zi % 2]
                y_ = yT[zi % 2]
                oks = [("o_acc", lt, h) for h in range(4)]
                P.tt(sqo[:], o_acc[:, lt], o_acc[:, lt], ALU.mult, r=oks, w=[sqo], eng="gpsimd")
                P.tt(nwz[:], z_[:], nw4[:], ALU.mult, r=[z_, nw4], w=[nwz], eng="gpsimd")
                yield
                P.op("vector", lambda e: e.tensor_reduce(out=ms4[:], in_=sqo[:], axis=mybir.AxisListType.X, op=ALU.add), r=[sqo], w=[ms4])
                yield
                P.act(ms4[:], ms4[:], AF.Ln, scale=1.0 / 128, bias=EPS, r=[ms4], w=[ms4])
                P.act(ms4[:], ms4[:], AF.Exp, scale=-0.5, r=[ms4], w=[ms4])
                yield
                P.tt(t1o[:], o_acc[:, lt], ms4[:].unsqueeze(2).broadcast_to([128, 4, 128]), ALU.mult, r=oks + [ms4], w=[t1o])
                P.tt(yb[:], t1o[:], nwz[:], ALU.mult, r=[t1o, nwz], w=[yb])
                yield
                pbf = P.psumb()
                for h in range(4):
                    P.tr(pbf[:, h * 128:(h + 1) * 128], yb[:, h, :], identb[:], r=[yb, identb], w=[pbf])
                P.cp(y_[:], v4(pbf), r=[pbf], w=[y_], eng="scalar")
                yield
                P.dma(ymv[:, 0:4, lt * 128:(lt + 1) * 128], y_[:], r=[y_], w=["ymixT_d"], lane="st_a%d" % (zi % 2))

            def advance(g, n=None):
                try:
                    next(g)
                    return True
                except StopIteration:
                    return False

            orders = [(0, [0, 1] + list(range(2, 2 + NOUT))), (1, [1, 0] + list(range(NTL - 1, 1, -1)))]
            HALF = 8
            gi = 0
            zi = 0
            pending_fin = None
            nd = lambda ti: (2 <= ti < 2 + NOUT)
            for d, order in orders:
                for h in range(4):
                    P.memset(S[h][:], 0.0, w=[S[h]])
                    P.memset(Sb[h][:], 0.0, w=[Sb[h]])
                n_o = len(order)
                g0 = gi
                mk = lambda i: prep(order[i], d, (g0 + i) % 3, (g0 + i) % 4, nd(order[i]), (g0 + i) % 2)
                for i in range(min(3, n_o)):
                    load(order[i], (g0 + i) % 4)
                gold = mk(0)
                while advance(gold):
                    pass
                gold = mk(1) if n_o > 1 else None
                if gold is not None:
                    for _ in range(HALF):
                        advance(gold)
                for i, ti in enumerate(order):
                    if i + 3 < n_o:
                        load(order[i + 3], (g0 + i + 3) % 4)
                    if d == 1 and nd(ti):
                        P.dma(zb[zi % 2][:].rearrange("p h i -> p (h i)"), z_d[(ti - 2) * 128:(ti - 1) * 128, :], r=["z_d"], w=[zb[zi % 2]], lane="ld_z%d" % (zi % 2))
                    gnew = mk(i + 2) if i + 2 < n_o else None
                    gsc = scan(ti, d, (g0 + i) % 3, (g0 + i) % 4, nd(ti))
                    act_ = {"sc": gsc, "old": gold, "new": gnew, "fin": pending_fin}
                    nnew = 0
                    while any(v is not None for k, v in act_.items() if k != "new") or (act_["new"] is not None and nnew < HALF):
                        for k in ["sc", "old", "new", "fin"]:
                            g = act_[k]
                            if g is None:
                                continue
                            if k == "new":
                                if nnew >= HALF:
                                    continue
                                nnew += 1
                            if not advance(g):
                                act_[k] = None
                                if k == "new":
                                    gnew = None
                    pending_fin = None
                    if d == 1 and nd(ti):
                        pending_fin = finalize(ti - 2, zi)
                        zi += 1
                    gold = gnew
                gi += n_o
            while pending_fin is not None and advance(pending_fin):
                pass
            if debug and "o_acc" in debug:
                dd = dbg_out("o_acc", [128, NOUT, 4, 128])
                P.dma(dd, o_acc[:], r=okeys, w=["dbg_o_acc"], lane="st_c")
                for nm in ["beta", "g", "gcum", "egt"]:
                    dd = dbg_out("g_" + nm, [128, 2, NTL, 4])
                    P.dma(dd, GA[nm][:], r=[GA[nm]], w=["dbg_g_" + nm], lane="st_c")
            P.finish()
            P.emit()
        if stop_after in ("S3", "S3a"):
            return nc, I, dbg

        with ExitStack() as st:
            P.st = st
            lcr = P.sb("s4_lcr", [24, 128], F32)
            lvr = P.sb("s4_lvr", [24, 128], F32)
            lc = P.sb("s4_lc", [128, 24], F32)
            lv = P.sb("s4_lv", [128, 24], F32)
            cd = P.sb("s4_cd", [128, 20, 128], BF16)
            Wg = P.sb("s4_Wg", [128, 16, 128], BF16)
            sc1 = P.sb("s4_sc1", [128, 8], F32)
            sc2 = P.sb("s4_sc2", [128, 8], F32)
            P.dma(lcr[:], lconv, r=[], w=[lcr], lane="ld_a")
            P.dma(lvr[:], lvec, r=[], w=[lvr], lane="ld_b")
            ps = P.psum()
            P.tr(ps[:, 0:24], lcr[:], ident[0:24, 0:24], r=[lcr, ident], w=[ps])
            P.cp(lc[:], ps[:, 0:24], r=[ps], w=[lc])
            ps = P.psum()
            P.tr(ps[:, 0:24], lvr[:], ident[0:24, 0:24], r=[lvr, ident], w=[ps])
            P.cp(lv[:], ps[:, 0:24], r=[ps], w=[lv])
            for i in range(20):
                P.ts(cd[:, i, :], ident[:], lc[:, i:i + 1], ALU.mult, r=[ident, lc], w=[cd], eng="gpsimd")
            P.memset(Wg[:], 0.0, w=[Wg])
            for d in range(2):
                for g in range(2):
                    for c in range(4):
                        for nb in range(2):
                            P.dma(Wg[nb * 64:(nb + 1) * 64, d * 8 + g * 4 + c, nb * 64:(nb + 1) * 64], lwg[d, g, 2 * c + nb],
                                  r=[], w=[Wg], lane="ld_w0", eng="gpsimd")
            P.act(sc1[:], lv[:, 16:24], AF.Exp, scale=-1.0, r=[lv], w=[sc1])
            P.act(sc1[:], sc1[:], AF.Ln, bias=1.0, r=[sc1], w=[sc1])
            P.ts(sc2[:], sc1[:], -16.0, ALU.mult, r=[sc1], w=[sc2])
            P.ts(sc1[:], sc1[:], -8.0, ALU.mult, r=[sc1], w=[sc1])
            if stop_after == "S4a":
                P.finish()
                P.emit()
                return nc, I, dbg
            raw_l = P.sb("s4_rawl", [128, SEQ], BF16)
            xcm = P.sb("s4_xcm", [128, SEQ + 4], BF16)
            xcp = P.sb("s4_xcp", [128, CTX + 4], BF16)
            xc = P.sb("s4_xc", [128, TALL], F32)
            xcb = P.sb("s4_xcb", [128, TALL], BF16)
            rr = P.sb("s4_rr", [128, TALL], F32)
            ii = P.sb("s4_ii", [128, TALL], F32)
            aa = P.sb("s4_aa", [128, TALL], F32)
            hd = [P.sb("s4_h%d" % d, [128, TALL], F32) for d in range(2)]
            rg = P.sb("s4_rg", [128, OWN], BF16)
            g1 = P.sb("s4_g1", [128, OWN], F32)
            g2 = P.sb("s4_g2", [128, OWN], F32)
            yl = P.sb("s4_yl", [128, OWN], BF16)
            P.memset(xcm[:, 0:2], 0.0, w=[xcm])
            P.op("gpsimd", lambda e: e.memset(xcm[:, SEQ + 2:SEQ + 4], 0.0), r=[xcm], w=[xcm])
            P.memset(xcp[:], 0.0, w=[xcp])
            blocks = [(0, CTX)] + [(CTX + i * 512, 512) for i in range(SEQ // 512)]
            def s4_front(c):
                P.dma(raw_l[:], pT_d[12 + c, :, 6 + CTX:6 + CTX + SEQ], r=["pT_d"], w=[raw_l], lane="ld_x0")
                P.dma(xcp[:, 2:2 + CTX], pT_d[12 + c, :, 2:2 + CTX], r=["pT_d"], w=[xcp], lane="ld_x1")
                P.cp(xcm[:, 2:2 + SEQ].rearrange("p (c r) -> p c r", r=64), raw_l[:].rearrange("p (r c) -> p c r", c=64),
                     r=[raw_l], w=[xcm], eng="vector")
                for bi, (b0, n) in enumerate(blocks):
                    ps = P.psum()
                    src = xcp if bi == 0 else xcm
                    o0 = 0 if bi == 0 else b0 - CTX
                    for k in range(5):
                        P.mm(ps[:, 0:n], lhsT=cd[:, c * 5 + k, :], rhs=src[:, o0 + k:o0 + k + n], start=(k == 0), stop=(k == 4), r=[cd, src], w=[ps])
                    P.act(xc[:, b0:b0 + n], ps[:, 0:n], AF.Identity, bias=lc[:, 20 + c:21 + c], r=[ps, lc], w=[xc])
                    P.cp(xcb[:, b0:b0 + n], xc[:, b0:b0 + n], r=[xc], w=[xcb])

            s4_front(0)
            for c in range(4):
                P.dma(rg[:], pT_d[16 + c, :, 6 + CTX:6 + CTX + OWN], r=["pT_d"], w=[rg], lane="ld_z0")
                for d in range(2):
                    wi = d * 8
                    for bi, (b0, n) in enumerate(blocks):
                        pr = P.psum()
                        pi_ = P.psum()
                        P.mm(pr[:, 0:n], lhsT=Wg[:, wi + c, :], rhs=xcb[:, b0:b0 + n], start=True, stop=True, r=[Wg, xcb], w=[pr])
                        P.mm(pi_[:, 0:n], lhsT=Wg[:, wi + 4 + c, :], rhs=xcb[:, b0:b0 + n], start=True, stop=True, r=[Wg, xcb], w=[pi_])
                        P.act(rr[:, b0:b0 + n], pr[:, 0:n], AF.Sigmoid, bias=lv[:, wi + c:wi + c + 1], r=[pr, lv], w=[rr])
                        P.act(ii[:, b0:b0 + n], pi_[:, 0:n], AF.Sigmoid, bias=lv[:, wi + 4 + c:wi + 4 + c + 1], r=[pi_, lv], w=[ii])
                    sci = d * 4 + c
                    P.tt(ii[:], ii[:], xc[:], ALU.mult, r=[ii, xc], w=[ii])
                    if d == 1 and c + 1 < 4:
                        s4_front(c + 1)
                    P.act(aa[:], rr[:], AF.Exp, scale=sc1[:, sci:sci + 1], r=[rr, sc1], w=[aa])
                    P.act(rr[:], rr[:], AF.Exp, scale=sc2[:, sci:sci + 1], r=[rr, sc2], w=[rr])
                    P.act(rr[:], rr[:], AF.Ln, scale=-1.0, bias=1.0, r=[rr], w=[rr])
                    P.act(rr[:], rr[:], AF.Exp, scale=0.5, r=[rr], w=[rr])
                    first = 0 if d == 0 else CTX - 1
                    P.op("gpsimd", lambda e, first=first: e.memset(rr[:, first:first + 1], 1.0), r=[rr], w=[rr])
                    P.tt(ii[:], ii[:], rr[:], ALU.mult, r=[ii, rr], w=[ii])
                    if stop_after == "S4c":
                        P.finish()
                        P.emit()
                        return nc, I, dbg
                    h_ = hd[d]
                    if d == 0:
                        P.op("vector", lambda e, h_=h_: e.tensor_tensor_scan(out=h_[:, 0:CTX], data0=aa[:, 0:CTX], data1=ii[:, 0:CTX],
                                                                             initial=0.0, op0=ALU.mult, op1=ALU.add), r=[aa, ii], w=[h_])
                        P.op("vector", lambda e, h_=h_: e.tensor_tensor_scan(out=h_[:, CTX:TALL], data0=aa[:, CTX:TALL], data1=ii[:, CTX:TALL],
                                                                             initial=h_[:, CTX - 1:CTX], op0=ALU.mult, op1=ALU.add), r=[aa, ii, h_], w=[h_])
                    else:
                        P.op("vector", lambda e, h_=h_: e.tensor_tensor_scan(out=h_[:, 0:CTX][:, ::-1], data0=aa[:, 0:CTX][:, ::-1], data1=ii[:, 0:CTX][:, ::-1],
                                                                             initial=0.0, op0=ALU.mult, op1=ALU.add), r=[aa, ii], w=[h_])
                        P.op("vector", lambda e, h_=h_: e.tensor_tensor_scan(out=h_[:, CTX:TALL][:, ::-1], data0=aa[:, CTX:TALL][:, ::-1], data1=ii[:, CTX:TALL][:, ::-1],
                                                                             initial=h_[:, 0:1], op0=ALU.mult, op1=ALU.add), r=[aa, ii, h_], w=[h_])
                if stop_after == "S4d":
                    P.finish()
                    P.emit()
                    return nc, I, dbg
                P.tt(hd[0][:, CTX:TALL], hd[0][:, CTX:TALL], hd[1][:, CTX:TALL], ALU.add, r=[hd[0], hd[1]], w=[hd[0]])
                P.tt(g1[:], rg[:], rg[:], ALU.mult, r=[rg], w=[g1], eng="gpsimd")
                P.ts(g1[:], g1[:], 0.044715, ALU.mult, s2=1.0, op1=ALU.add, r=[g1], w=[g1])
                P.tt(g1[:], g1[:], rg[:], ALU.mult, r=[g1, rg], w=[g1])
                P.act(g1[:], g1[:], AF.Sigmoid, scale=1.5957691216057308, r=[g1], w=[g1])
                P.tt(g2[:], g1[:], rg[:], ALU.mult, r=[g1, rg], w=[g2], eng="gpsimd")
                hv = hd[0][:, CTX:TALL].rearrange("p (c r) -> p r c", r=64)[:, 0:OWN // 64, :]
                P.tt(yl[:].rearrange("p (r c) -> p r c", c=64), g2[:].rearrange("p (r c) -> p r c", c=64), hv, ALU.mult, r=[g2, hd[0]], w=[yl])
                P.dma(ymixT_d[4 + c], yl[:], r=[yl], w=["ymixT_d"], lane="st_b0")
            P.finish()
            P.emit()
        if stop_after == "S4":
            return nc, I, dbg

        with ExitStack() as st:
            P.st = st
            w1b = P.sb("s6_w1b", [128, KC, FF], BF16)
            w3b = P.sb("s6_w3b", [128, KC, FF], BF16)
            w2b = P.sb("s6_w2b", [128, FC, D], BF16)
            st_a = ExitStack()
            P.st = st_a
            woutb = P.sb("s5_wout", [128, KC, D], BF16)
            wov = w_out.rearrange("(kc p) n -> p kc n", p=128)
            for kc in range(KC):
                P.dma(woutb[:, kc, :], wov[:, kc, :], r=[], w=[woutb], lane="ld_w3", eng="gpsimd")
            load_ffn_weights(1, w1b, w3b, w2b)
            ymt = [P.sb("s5_ym%d" % i, [128, KC, 128], BF16) for i in range(2)]
            hx = [P.sb("s5_hx%d" % i, [128, D], F32) for i in range(2)]
            tmp5 = P.sb("s5_tmp", [128, 512], F32)
            ymv5 = ymixT_d.rearrange("c p t -> p c t")
            for ti in range(OWN // 128):
                ym_, hx_ = ymt[ti % 2], hx[ti % 2]
                P.dma(ym_[:], ymv5[:, :, ti * 128:(ti + 1) * 128], r=["ymixT_d"], w=[ym_], lane="ld_x%d" % (ti % 2))
                P.dma(hx_[:], h1_d[ti * 128:(ti + 1) * 128, :], r=["h1_d"], w=[hx_], lane="ld_z%d" % (ti % 2))
                for hh in range(2):
                    po = P.psum()
                    for kc in range(KC):
                        P.mm(po[:], lhsT=ym_[:, kc, :], rhs=woutb[:, kc, hh * 512:(hh + 1) * 512], start=(kc == 0), stop=(kc == KC - 1), r=[ym_, woutb], w=[po])
                    P.tt(tmp5[:], po[:], gate_b[:, 2, hh * 512:(hh + 1) * 512], ALU.mult, r=[po, gate_b], w=[tmp5])
                    P.tt(hx_[:, hh * 512:(hh + 1) * 512], hx_[:, hh * 512:(hh + 1) * 512], tmp5[:], ALU.add, r=[hx_, tmp5], w=[hx_])
                P.dma(h2_d[ti * 128:(ti + 1) * 128, :], hx_[:], r=[hx_], w=["h2_d"], lane="st_a%d" % (ti % 2))
            P.finish(skip=("ld_w0", "ld_w1", "ld_w2"))
            P.emit()
            st_a.close()
            P.st = st
            if stop_after == "S5a":
                return nc, I, dbg

            nsub = NT // 128
            xss6 = [[P.sb("s6_x%d_%d" % (q_, j), [128, D], F32) for j in range(nsub)] for q_ in range(2)]
            xn = [P.sb("s6_xn%d" % j, [128, D], F32) for j in range(nsub)]
            ssq2 = P.sb("s6_ssq2", [128, 4], F32)
            rstd2 = P.sb("s6_rstd2", [128, 4], F32)
            uT = P.sb("s6_uT", [128, KC, NT], BF16)
            hT = P.sb("s6_hT", [128, FC, NT], BF16)
            sg = [P.sb("s6_sg%d" % j, [128, NT], F32) for j in range(2)]
            tmp = P.sb("s6_tmp", [128, 512], F32)
            ssq = P.sb("s6_ssq", [128, 4], F32)
            rstd = P.sb("s6_rstd", [128, 4], F32)
            fgb = gate_b[:, 0, :]
            P.dma(fgb, bass.AP(final_g.tensor, 0, [[0, 128], [1, D]]), r=[], w=[gate_b], lane="ld_a")

            def s6_load(ti_):
                for j in range(nsub):
                    P.dma(xss6[ti_ % 2][j][:], h2_d[ti_ * NT + j * 128:ti_ * NT + (j + 1) * 128, :], r=["h2_d"], w=[xss6[ti_ % 2][j]],
                          lane="ld_x%d_%d" % (ti_ % 2, j))
            n6 = OWN // NT
            s6_load(0)
            norm_mod_T(xss6[0], nsub, 2, 0, xn, uT, ssq, rstd)
            for ti in range(n6):
                t0 = ti * NT
                xs = xss6[ti % 2]
                hook = None
                if ti + 1 < n6:
                    s6_load(ti + 1)
                    hook = norm_gen(xss6[(ti + 1) % 2], nsub, 2, 0, xn, uT, ssq, rstd)
                ffn_core(xs, nsub, uT, hT, w1b, w3b, w2b, 3, sg, tmp, hook=hook)
                if hook is not None:
                    for _ in hook:
                        pass
                for j in range(nsub):
                    P.act(xn[j][:], xs[j][:], AF.Square, accum_out=ssq2[:, j:j + 1], r=[xs[j]], w=[xn[j], ssq2])
                P.act(rstd2[:, 0:nsub], ssq2[:, 0:nsub], AF.Ln, scale=1.0 / D, bias=EPS, r=[ssq2], w=[rstd2])
                P.act(rstd2[:, 0:nsub], rstd2[:, 0:nsub], AF.Exp, scale=-0.5, r=[rstd2], w=[rstd2])
                for j in range(nsub):
                    P.stt(xn[j][:], in0=xs[j][:], scalar=rstd2[:, j:j + 1], in1=fgb, op0=ALU.mult, op1=ALU.mult, r=[xs[j], rstd2, gate_b], w=[xn[j]])
                    P.dma(y[t0 + j * 128:t0 + (j + 1) * 128, :], xn[j][:], r=[xn[j]], w=["y"], lane="st_a%d" % j)
            P.finish()
            P.emit()
    return nc, I, dbg


def _offmasks():
    ii = np.arange(128)[:, None]
    jj = np.arange(128)[None, :]
    ms = []
    for li in range(7):
        s_ = 1 << li
        m_ = ((ii // (2 * s_)) == (jj // (2 * s_))) & ((ii % (2 * s_)) >= s_) & ((jj % (2 * s_)) < s_)
        ms.append(m_.astype(np.float32))
        ms.append(m_.T.astype(np.float32))
    return np.ascontiguousarray(np.stack(ms, 0))


OFFM = _offmasks()


def make_in_maps(inputs):
    f = lambda a: np.ascontiguousarray(np.asarray(a, dtype=np.float32))
    maps = []
    for core in range(8):
        b, half = core // 2, core % 2
        xb = f(inputs["x"][b])
        cb = f(inputs["ctx"][b])
        if half == 1:
            xb = f(xb[::-1])
            cb = f(cb[::-1])
        cc = np.concatenate([f(inputs["c"][b]).reshape(8, 128), f(inputs["c_ctx"]).reshape(8, 128)], 0)
        m = {
            "x": xb, "ctx": cb, "cc": f(cc),
            "w_ada": f(inputs["w_ada"][0]), "b_ada": f(inputs["b_ada"][0]).reshape(1, -1),
            "norm_g": f(inputs["norm_g"][0]).reshape(24, 128),
            "w1": f(inputs["ffn_w1"][0]), "w3": f(inputs["ffn_w3"][0]), "w2": f(inputs["ffn_w2"][0]),
            "w_in": f(inputs["w_in"][0]), "w_out": f(inputs["w_out"][0]),
            "final_g": f(inputs["final_norm_g"]),
        }
        rev = (half == 1)
        def taps5(w):
            z = np.zeros((1, w.shape[1]), np.float32)
            return np.concatenate([z, w[::-1]], 0) if rev else np.concatenate([w, z], 0)
        g5 = taps5(f(inputs["gdn_conv_w"][0]))
        m["gconv"] = f(g5.reshape(5, 12, 128).transpose(1, 0, 2).reshape(60, 128))
        al = f(inputs["gdn_a_log"][0]); dtb = f(inputs["gdn_dt_bias"][0])
        if rev:
            al = al[::-1]; dtb = dtb[::-1]
        m["gvec"] = f(np.concatenate([al.reshape(-1), dtb.reshape(-1)]))
        m["gnw"] = f(inputs["gdn_norm_w"][0])
        l5 = taps5(f(inputs["lru_conv_w"][0]))
        m["lconv"] = f(np.concatenate([l5.reshape(5, 4, 128).transpose(1, 0, 2).reshape(20, 128),
                                       f(inputs["lru_conv_b"][0]).reshape(4, 128)], 0))
        lw = f(inputs["lru_w_gate"][0]); lb = f(inputs["lru_b_gate"][0]); ll = f(inputs["lru_lambda"][0])
        if rev:
            lw = lw[::-1]; lb = lb[::-1]; ll = ll[::-1]
            wi = m["w_in"].copy()
            o = 2048
            wi[:, o:o + 4], wi[:, o + 4:o + 8] = m["w_in"][:, o + 4:o + 8], m["w_in"][:, o:o + 4]
            wi[:, o + 8:o + 12], wi[:, o + 12:o + 16] = m["w_in"][:, o + 12:o + 16], m["w_in"][:, o + 8:o + 12]
            m["w_in"] = f(wi)
        m["lwg"] = f(lw)
        m["offm"] = OFFM
        m["lvec"] = f(np.concatenate([lb.reshape(16, 128), ll.reshape(8, 128)], 0))
        maps.append(m)
    return maps


def kernel(**inputs):
    nc, I, dbg = build_nc()
    maps = make_in_maps(inputs)
    maps = [{k: v for k, v in m.items() if k in I} for m in maps]
    res = run_bass_kernel_spmd(nc, maps, core_ids=list(range(8)))
    out = np.zeros((4, SEQ, D), np.float32)
    for core in range(8):
        b, half = core // 2, core % 2
        yy = np.asarray(res.results[core]["y"])
        if half == 0:
            out[b, :OWN] = yy
        else:
            out[b, OWN:] = yy[::-1]
    return out
```

```python
import numpy as np
from contextlib import ExitStack
import concourse.bass as bass
import concourse.mybir as mybir
from concourse.bass_utils import run_bass_kernel_spmd

F32 = mybir.dt.float32
BF16 = mybir.dt.bfloat16
AF = mybir.ActivationFunctionType
ALU = mybir.AluOpType
ENGS = ["tensor", "vector", "scalar", "gpsimd", "sync"]

D = 1024
KC = 8
FF = 2816
FC = 22
SEQ = 4096
CTX = 256
TALL = CTX + SEQ
OWN = 2048
NMOD = 9
EPS = 1e-6
NT = 256
IN_COLS = 3088
TP = TALL + 8


def pcol(t):
    return 2 + t if t < CTX else 6 + t


class Prog:
    def __init__(self, nc, es):
        self.nc = nc
        self.es = es
        self.q = {e: [] for e in ENGS}
        self.cnt = {}
        self.sems = {}
        self.last_write = {}
        self.readers = {}
        self.knows = {e: {} for e in ENGS}
        self.nops = 0
        self._psum = []
        self._psi = 0
        self._pcnt = {}
        self.st = None

    def sem(self, key):
        if key not in self.sems:
            self.sems[key] = self.es.enter_context(self.nc.semaphore("s_" + key))
            self.cnt[key] = 0
        return self.sems[key]

    def gsb(self, name, shape, dt=F32):
        return self.es.enter_context(self.nc.sbuf_tensor(name, list(shape), dt))

    def sb(self, name, shape, dt=F32):
        return self.st.enter_context(self.nc.sbuf_tensor(name, list(shape), dt))

    def psum_pool(self, n=8, nb=0):
        self._psum = [self.es.enter_context(self.nc.psum_tensor("psb%d" % i, [128, 512], F32)) for i in range(n)]
        self._psumb = [self.es.enter_context(self.nc.psum_tensor("psbf%d" % i, [128, 1024], BF16)) for i in range(nb)]
        self._psbi = 0

    def psumb(self):
        t = self._psumb[self._psbi % len(self._psumb)]
        self._psbi += 1
        return t

    def psum(self, pool=None):
        if pool is None:
            t = self._psum[self._psi % len(self._psum)]
            self._psi += 1
            return t
        idx = {"a": [0, 1, 2, 3], "b": [4, 5]}[pool]
        c = self._pcnt.get(pool, 0)
        self._pcnt[pool] = c + 1
        return self._psum[idx[c % len(idx)]]

    def _need(self, eng, waits, tok, same_ok):
        semkey, val, peng = tok
        if peng == eng and same_ok:
            return
        if peng == "dma":
            val = self.cnt[semkey]
        if self.knows[eng].get(semkey, 0) >= val:
            return
        waits[semkey] = max(waits.get(semkey, 0), val)

    def _key(self, k):
        if isinstance(k, (str, tuple)):
            return k
        return k.name

    def op(self, eng, fn, r=(), w=(), lane=None):
        r = [self._key(k) for k in r]
        w = [self._key(k) for k in w]
        waits = {}
        is_dma = lane is not None
        for k in r:
            if k in self.last_write:
                self._need(eng, waits, self.last_write[k], same_ok=(eng == "tensor" and not is_dma))
            if isinstance(k, str) and k.startswith("psb"):
                for tok in self.readers.get(k, ()):
                    self._need(eng, waits, tok, same_ok=True)
        for k in w:
            if k in self.last_write:
                self._need(eng, waits, self.last_write[k], same_ok=not is_dma)
            for tok in self.readers.get(k, ()):
                self._need(eng, waits, tok, same_ok=not is_dma)
        for sk, v in waits.items():
            self.knows[eng][sk] = v
        if is_dma:
            self.sem(lane)
            self.cnt[lane] += 16
            tok = (lane, self.cnt[lane], "dma")
        else:
            self.sem(eng)
            self.cnt[eng] += 1
            tok = (eng, self.cnt[eng], eng)
        for k in w:
            self.last_write[k] = tok
            self.readers[k] = []
        for k in r:
            self.readers.setdefault(k, []).append(tok)
        self.q[eng].append((waits, fn, tok, is_dma))
        self.nops += 1
        return tok

    def finish(self, eng="sync", skip=()):
        waits = {}
        for sk, c in self.cnt.items():
            if sk in skip:
                continue
            if c > 0 and self.knows[eng].get(sk, 0) < c:
                waits[sk] = c
                self.knows[eng][sk] = c
        self.q[eng].append((waits, None, None, False))

    def emit(self):
        nc = self.nc
        q = self.q
        self.q = {e: [] for e in ENGS}
        with nc.Block() as block:
            def mk(ename):
                def body(e):
                    for waits, fn, tok, is_dma in q[ename]:
                        for sk, v in waits.items():
                            e.wait_ge(self.sems[sk], v)
                        if fn is None:
                            continue
                        ins = fn(e)
                        ins.then_inc(self.sems[tok[0]], 16 if is_dma else 1)
                return body
            block.tensor(mk("tensor"))
            block.vector(mk("vector"))
            block.scalar(mk("scalar"))
            block.gpsimd(mk("gpsimd"))
            block.sync(mk("sync"))

    def mm(self, out, lhsT, rhs, start, stop, r, w):
        return self.op("tensor", lambda e: e.matmul(out, lhsT=lhsT, rhs=rhs, start=start, stop=stop), r=r, w=w)

    def tr(self, out, in_, ident, r, w):
        return self.op("tensor", lambda e: e.transpose(out, in_, ident), r=r, w=w)

    def act(self, out, in_, func, r, w, scale=None, bias=None, accum_out=None, eng="scalar"):
        kw = {}
        if scale is not None:
            kw["scale"] = scale
        if bias is not None:
            kw["bias"] = bias
        if accum_out is not None:
            kw["accum_out"] = accum_out
        return self.op("scalar", lambda e: e.activation(out=out, in_=in_, func=func, **kw), r=r, w=w)

    def tt(self, out, in0, in1, op, r, w, eng="vector"):
        return self.op(eng, lambda e: e.tensor_tensor(out=out, in0=in0, in1=in1, op=op), r=r, w=w)

    def ts(self, out, in0, s1, op0, r, w, s2=None, op1=None, eng="vector"):
        if op1 is None:
            return self.op(eng, lambda e: e.tensor_scalar(out=out, in0=in0, scalar1=s1, scalar2=None, op0=op0), r=r, w=w)
        return self.op(eng, lambda e: e.tensor_scalar(out=out, in0=in0, scalar1=s1, scalar2=s2, op0=op0, op1=op1), r=r, w=w)

    def stt(self, out, in0, scalar, in1, op0, op1, r, w):
        return self.op("vector", lambda e: e.scalar_tensor_tensor(out=out, in0=in0, scalar=scalar, in1=in1, op0=op0, op1=op1), r=r, w=w)

    def cp(self, out, in_, r, w, eng="vector"):
        if eng == "scalar":
            return self.op("scalar", lambda e: e.copy(out=out, in_=in_), r=r, w=w)
        return self.op(eng, lambda e: e.tensor_copy(out=out, in_=in_), r=r, w=w)

    def memset(self, ap, val, w, eng="gpsimd"):
        return self.op(eng, lambda e: e.memset(ap, val), w=w)

    def dma(self, out, in_, r, w, lane, eng="sync"):
        return self.op(eng, lambda e: e.dma_start(out=out, in_=in_), r=r, w=w, lane=lane)


def bcast_rows(t_ap, n_part, offset, length):
    return bass.AP(t_ap.tensor, offset, [[0, n_part], [1, length]])


def build_nc(debug=None, stop_after=None):
    nc = bass.Bass("TRN2", target_bir_lowering=False)
    I = {}

    def inp(name, shape):
        I[name] = nc.dram_tensor(name, list(shape), F32, kind="ExternalInput").ap()
        return I[name]

    x = inp("x", [SEQ, D])
    ctx = inp("ctx", [CTX, D])
    cc = inp("cc", [16, 128])
    w_ada = inp("w_ada", [D, NMOD * D])
    b_ada = inp("b_ada", [1, NMOD * D])
    norm_g = inp("norm_g", [24, 128])
    w1 = inp("w1", [2, D, FF])
    w3 = inp("w3", [2, D, FF])
    w2 = inp("w2", [2, FF, D])
    w_in = inp("w_in", [D, IN_COLS])
    w_out = inp("w_out", [D, D])
    final_g = inp("final_g", [D])
    gconv = inp("gconv", [60, 128])
    gvec = inp("gvec", [16])
    gnw = inp("gnw", [128])
    lconv = inp("lconv", [24, 128])
    lwg = inp("lwg", [2, 2, 8, 64, 64])
    lvec = inp("lvec", [24, 128])
    offm = inp("offm", [14, 128, 128])
    y = nc.dram_tensor("y", [OWN, D], F32, kind="ExternalOutput").ap()

    dbg = {}

    def dbg_out(name, shape, dt=F32):
        dbg[name] = nc.dram_tensor("dbg_" + name if not name.endswith("_d") else name, list(shape), dt, kind="ExternalOutput").ap()
        return dbg[name]

    def scratch(name, shape, dt):
        if debug and name in debug:
            return dbg_out(name, shape, dt)
        return nc.dram_tensor(name, list(shape), dt).ap()

    u2T_d = scratch("u2T_d", [KC, 128, TALL], BF16)
    h1_d = scratch("h1_d", [OWN, D], F32)
    pT_d = scratch("pT_d", [20, 128, TP], BF16)
    z_d = scratch("z_d", [OWN, 512], F32)
    ba_d = scratch("ba_d", [TALL, 16], F32)
    qkvT_d = scratch("qkvT_d", [12, 128, TALL], BF16)
    ymixT_d = scratch("ymixT_d", [KC, 128, OWN], BF16)
    h2_d = scratch("h2_d", [OWN, D], F32)

    with ExitStack() as es:
        P = Prog(nc, es)
        P.psum_pool(6, 2)
        ident = P.gsb("ident", [128, 128], F32)
        modcol = P.gsb("modcol", [128, NMOD * KC, 2], F32)
        gcol = P.gsb("gcol", [128, 24], F32)
        gscol = P.gsb("gscol", [128, 3, 2, KC], F32)
        gate_b = P.gsb("gate_b", [128, 4, D], F32)
        sel = P.gsb("sel", [2, 2, 128], F32)
        id2 = P.gsb("id2", [2, 2], F32)

        def ident_setup():
            P.memset(ident[:], 1.0, w=[ident])
            P.op("gpsimd", lambda e: e.affine_select(out=ident[:], in_=ident[:], pattern=[[-1, 128]],
                                                      compare_op=ALU.is_equal, fill=0.0, base=0, channel_multiplier=1),
                 r=[ident], w=[ident])
            P.memset(sel[:], 0.0, w=[sel])
            P.op("gpsimd", lambda e: e.memset(sel[0:1, 0, :], 1.0), r=[sel], w=[sel])
            P.op("gpsimd", lambda e: e.memset(sel[:, 1, :], 1.0), r=[sel], w=[sel])
            P.op("gpsimd", lambda e: e.affine_select(out=sel[:, 1, :], in_=sel[:, 1, :], pattern=[[0, 128]],
                                                      compare_op=ALU.is_equal, fill=0.0, base=-1, channel_multiplier=1),
                 r=[sel], w=[sel])
            P.cp(id2[:], ident[0:2, 0:2], r=[ident], w=[id2], eng="gpsimd")

        def load_ffn_weights(li, w1b, w3b, w2b):
            w1v = w1[li].rearrange("(kc p) n -> p kc n", p=128)
            w3v = w3[li].rearrange("(kc p) n -> p kc n", p=128)
            w2v = w2[li].rearrange("(fc p) n -> p fc n", p=128)
            hf = FF // 2
            for kc in range(KC):
                for h2 in range(2):
                    P.dma(w1b[:, kc, h2 * hf:(h2 + 1) * hf], w1v[:, kc, h2 * hf:(h2 + 1) * hf], r=[], w=[w1b], lane="ld_w0", eng="gpsimd")
                    P.dma(w3b[:, kc, h2 * hf:(h2 + 1) * hf], w3v[:, kc, h2 * hf:(h2 + 1) * hf], r=[], w=[w3b], lane="ld_w1", eng="gpsimd")
            for fc in range(FC):
                P.dma(w2b[:, fc, :], w2v[:, fc, :], r=[], w=[w2b], lane="ld_w2", eng="gpsimd")

        with ExitStack() as st:
            P.st = st
            ident_setup()
            ccs = P.sb("s0_cc", [16, 128], F32)
            scT = P.sb("s0_scT", [128, 16], F32)
            ngs = P.sb("s0_ng", [24, 128], F32)
            bada = P.sb("s0_bada", [1, NMOD * D], F32)
            ones2 = P.sb("s0_ones2", [1, 2], F32)
            modrow = P.sb("s0_modrow", [2, NMOD * D], F32)
            wbuf = [P.sb("s0_wb%d" % i, [128, KC, 512], F32) for i in range(2)]
            P.dma(ccs[:], cc, r=[], w=[ccs], lane="ld_a")
            P.dma(ngs[:], norm_g, r=[], w=[ngs], lane="ld_a")
            P.dma(bada[:], b_ada, r=[], w=[bada], lane="ld_a")
            P.memset(ones2[:], 1.0, w=[ones2])
            P.act(ccs[:], ccs[:], AF.Silu, r=[ccs], w=[ccs])
            ps = P.psum()
            P.tr(ps[:, 0:16], ccs[:], ident[0:16, 0:16], r=[ccs, ident], w=[ps])
            P.cp(scT[:], ps[:, 0:16], r=[ps], w=[scT])
            ps = P.psum()
            P.tr(ps[:, 0:24], ngs[:], ident[0:24, 0:24], r=[ngs, ident], w=[ps])
            P.cp(gcol[:], ps[:, 0:24], r=[ps], w=[gcol])
            wv = w_ada.rearrange("(kc p) n -> p kc n", p=128)
            NB = NMOD * D // 512
            for nb in range(NB):
                wb = wbuf[nb % 2]
                P.dma(wb[:], wv[:, :, nb * 512:(nb + 1) * 512], r=[], w=[wb], lane="ld_w%d" % (nb % 2))
                ps = P.psum()
                for kc in range(KC):
                    P.mm(ps[0:2, :], lhsT=scT[:, kc::8], rhs=wb[:, kc, :], start=(kc == 0), stop=False, r=[scT, wb], w=[ps])
                P.mm(ps[0:2, :], lhsT=ones2[:], rhs=bada[:, nb * 512:(nb + 1) * 512], start=False, stop=True, r=[ones2, bada], w=[ps])
                P.cp(modrow[:, nb * 512:(nb + 1) * 512], ps[0:2, :], r=[ps], w=[modrow], eng="scalar")
            ps = P.psum()
            for j in range(NMOD * KC):
                P.mm(ps[:, 2 * j:2 * j + 2], lhsT=modrow[:, j * 128:(j + 1) * 128], rhs=id2[:], start=True, stop=True,
                     r=[modrow, id2], w=[ps])
            P.cp(modcol[:].rearrange("p j s -> p (j s)"), ps[:, 0:2 * NMOD * KC], r=[ps], w=[modcol])
            for i in range(3):
                for s in range(2):
                    P.stt(gscol[:, i, s, :], in0=modcol[:, (3 * i + 1) * KC:(3 * i + 2) * KC, s], scalar=1.0,
                          in1=gcol[:, i * KC:(i + 1) * KC], op0=ALU.add, op1=ALU.mult, r=[modcol, gcol], w=[gscol])
            for gi, (mod, s, sc) in enumerate([(2, 0, 0.5), (2, 1, 0.5), (5, 0, 1.0), (8, 0, 0.5)]):
                for hh in range(2):
                    ps = P.psum()
                    P.mm(ps[:], lhsT=sel[:, s, :], rhs=modrow[:, mod * D + hh * 512: mod * D + (hh + 1) * 512],
                         start=True, stop=True, r=[sel, modrow], w=[ps])
                    P.act(gate_b[:, gi, hh * 512:(hh + 1) * 512], ps[:], AF.Identity, scale=sc, r=[ps], w=[gate_b])
            if debug and "modrow" in debug:
                d = dbg_out("modrow", [2, NMOD * D])
                P.dma(d, modrow[:], r=[modrow], w=["dbg_modrow"], lane="st_a")
                d = dbg_out("gate_b", [128, 4, D])
                P.dma(d, gate_b[:], r=[gate_b], w=["dbg_gate_b"], lane="st_a")
                d = dbg_out("gscol", [128, 3, 2, KC])
                P.dma(d, gscol[:], r=[gscol], w=["dbg_gscol"], lane="st_a")
            P.finish()
            P.emit()

        if stop_after == "S0":
            return nc, I, dbg

        def norm_gen(xs, nsub, i, s, xn, uT, ssq, rstd, dst=None, dkey=None):
            if dst is None:
                dst = lambda kc: uT[:, kc, 0:nsub * 128]
                dkey = lambda kc: uT.name
            for j in range(nsub):
                P.act(xn[j][:], xs[j][:], AF.Square, accum_out=ssq[:, j:j + 1], r=[xs[j]], w=[xn[j], ssq])
            P.act(rstd[:, 0:nsub], ssq[:, 0:nsub], AF.Ln, scale=1.0 / D, bias=EPS, r=[ssq], w=[rstd])
            P.act(rstd[:, 0:nsub], rstd[:, 0:nsub], AF.Exp, scale=-0.5, r=[rstd], w=[rstd])
            for j in range(nsub):
                P.ts(xn[j][:], xs[j][:], rstd[:, j:j + 1], ALU.mult, r=[xs[j], rstd], w=[xn[j]])
            yield
            for kc in range(KC):
                ps = P.psum()
                for j in range(nsub):
                    P.tr(ps[:, j * 128:(j + 1) * 128], xn[j][:, kc * 128:(kc + 1) * 128], ident[:], r=[xn[j], ident], w=[ps])
                P.act(dst(kc), ps[:, 0:nsub * 128], AF.Identity,
                      scale=gscol[:, i, s, kc:kc + 1], bias=modcol[:, (3 * i) * KC + kc, s:s + 1],
                      r=[ps, gscol, modcol], w=[dkey(kc)])

        def norm_mod_T(xs, nsub, i, s, xn, uT, ssq, rstd):
            for _ in norm_gen(xs, nsub, i, s, xn, uT, ssq, rstd):
                pass

        def ffn_core(xs, nsub, uT, hT, w1b, w3b, w2b, gate_idx, sg, tmp, hook=None, hook0=None):
            n = nsub * 128
            for fc in range(FC):
                if hook0 is not None and fc == 2:
                    for _ in hook0:
                        pass
                if hook is not None and fc == FC - 2:
                    next(hook, None)
                p1 = P.psum()
                p3 = P.psum()
                for kc in range(KC):
                    P.mm(p1[:, 0:n], lhsT=w1b[:, kc, fc * 128:(fc + 1) * 128], rhs=uT[:, kc, 0:n], start=(kc == 0), stop=(kc == KC - 1),
                         r=[w1b, uT], w=[p1])
                for kc in range(KC):
                    P.mm(p3[:, 0:n], lhsT=w3b[:, kc, fc * 128:(fc + 1) * 128], rhs=uT[:, kc, 0:n], start=(kc == 0), stop=(kc == KC - 1),
                         r=[w3b, uT], w=[p3])
                s_ = sg[fc % 2]
                P.act(s_[:, 0:n], p1[:, 0:n], AF.Silu, r=[p1], w=[s_])
                P.tt(hT[:, fc, 0:n], s_[:, 0:n], p3[:, 0:n], ALU.mult, r=[s_, p3], w=[(hT.name, fc)])
            for j in range(nsub):
                for hh in range(2):
                    po = P.psum()
                    for fc in range(FC):
                        P.mm(po[:], lhsT=hT[:, fc, j * 128:(j + 1) * 128], rhs=w2b[:, fc, hh * 512:(hh + 1) * 512],
                             start=(fc == 0), stop=(fc == FC - 1), r=[(hT.name, fc), w2b], w=[po])
                    P.tt(tmp[:], po[:], gate_b[:, gate_idx, hh * 512:(hh + 1) * 512], ALU.mult, r=[po, gate_b], w=[tmp])
                    P.tt(xs[j][:, hh * 512:(hh + 1) * 512], xs[j][:, hh * 512:(hh + 1) * 512], tmp[:], ALU.add, r=[xs[j], tmp], w=[xs[j]])
                    if hook is not None and j == 0 and hh == 0:
                        next(hook, None)

        with ExitStack() as st:
            P.st = st
            w1b = P.sb("s1_w1b", [128, KC, FF], BF16)
            w3b = P.sb("s1_w3b", [128, KC, FF], BF16)
            w2b = P.sb("s1_w2b", [128, FC, D], BF16)
            load_ffn_weights(0, w1b, w3b, w2b)
            nsub = NT // 128
            xss = [[P.sb("s1_x%d_%d" % (q_, j), [128, D], F32) for j in range(nsub)] for q_ in range(2)]
            xn = [P.sb("s1_xn%d" % j, [128, D], F32) for j in range(nsub)]
            uT = P.sb("s1_uT", [128, KC, NT], BF16)
            hT = P.sb("s1_hT", [128, FC, NT], BF16)
            sg = [P.sb("s1_sg%d" % j, [128, NT], F32) for j in range(2)]
            tmp = P.sb("s1_tmp", [128, 512], F32)
            ssq = P.sb("s1_ssq", [128, 4], F32)
            rstd = P.sb("s1_rstd", [128, 4], F32)
            ntiles = TALL // NT
            if stop_after == "S1a":
                ntiles = 3
            def s1_load(ti_):
                t0_ = ti_ * NT
                for j in range(nsub):
                    src = ctx[t0_ + j * 128: t0_ + (j + 1) * 128, :] if t0_ < CTX else x[t0_ - CTX + j * 128: t0_ - CTX + (j + 1) * 128, :]
                    P.dma(xss[ti_ % 2][j][:], src, r=[], w=[xss[ti_ % 2][j]], lane="ld_x%d_%d" % (ti_ % 2, j))
            s1_load(0)
            s_of = lambda ti_: 1 if ti_ * NT < CTX else 0
            ssq2 = P.sb("s1_ssq2", [128, 4], F32)
            rstd2 = P.sb("s1_rstd2", [128, 4], F32)
            norm_mod_T(xss[0], nsub, 0, s_of(0), xn, uT, ssq, rstd)
            pending_n2 = None
            for ti in range(ntiles):
                t0 = ti * NT
                is_ctx = t0 < CTX
                s = 1 if is_ctx else 0
                xs = xss[ti % 2]
                hook = None
                if ti + 1 < ntiles:
                    s1_load(ti + 1)
                    hook = norm_gen(xss[(ti + 1) % 2], nsub, 0, s_of(ti + 1), xn, uT, ssq, rstd)
                ffn_core(xs, nsub, uT, hT, w1b, w3b, w2b, 1 if is_ctx else 0, sg, tmp, hook=hook, hook0=pending_n2)
                pending_n2 = None
                if hook is not None:
                    for _ in hook:
                        pass
                lat0 = t0 - CTX
                if (not is_ctx) and lat0 < OWN:
                    for j in range(nsub):
                        P.dma(h1_d[lat0 + j * 128: lat0 + (j + 1) * 128, :], xs[j][:], r=[xs[j]], w=["h1_d"], lane="st_a%d_%d" % (ti % 2, j))
                g2 = norm_gen(xs, nsub, 1, s, xn, uT, ssq2, rstd2,
                              dst=lambda kc: hT[:, FC - KC + kc, 0:nsub * 128], dkey=lambda kc: (hT.name, FC - KC + kc))
                next(g2)

                def n2_tail(g_, t0_):
                    for _ in g_:
                        pass
                    P.dma(u2T_d[:, :, t0_:t0_ + NT].rearrange("k p t -> p k t"), hT[:, FC - KC:FC, :],
                          r=[(hT.name, FC - KC + kc) for kc in range(KC)], w=["u2T_d"], lane="st_b")
                    yield
                pending_n2 = n2_tail(g2, t0)
            for _ in pending_n2:
                pass
            P.finish()
            P.emit()
        if stop_after in ("S1", "S1a"):
            return nc, I, dbg

        FM_COLS = [c * 128 for c in range(12)] + [2064 + c * 128 for c in range(8)]
        with ExitStack() as st:
            P.st = st
            winb = P.sb("s2_winb", [128, KC, IN_COLS], BF16)
            wiv = w_in.rearrange("(kc p) n -> p kc n", p=128)
            hc = IN_COLS // 2
            for kc in range(KC):
                for h2 in range(2):
                    P.dma(winb[:, kc, h2 * hc:(h2 + 1) * hc], wiv[:, kc, h2 * hc:(h2 + 1) * hc], r=[], w=[winb], lane="ld_w0", eng="gpsimd")
            u2 = [P.sb("s2_u%d" % i, [128, KC, 512], BF16) for i in range(2)]
            pst = [P.sb("s2_pst%d" % i, [128, 20, 512], BF16) for i in range(2)]
            zst = [P.sb("s2_z%d" % i, [128, 512], F32) for i in range(2)]
            bast = [P.sb("s2_ba%d" % i, [128, 16], F32) for i in range(2)]
            zer = P.sb("s2_zer", [128, 20, 4], BF16)
            P.memset(zer[:], 0.0, w=[zer])
            pv = pT_d.rearrange("c p t -> p c t")
            P.dma(pv[:, :, 0:2], zer[:, :, 0:2], r=[zer], w=["pT_d"], lane="st_c")
            P.dma(pv[:, :, 2 + CTX:6 + CTX], zer[:, :, 0:4], r=[zer], w=["pT_d"], lane="st_c")
            P.dma(pv[:, :, TP - 2:TP], zer[:, :, 0:2], r=[zer], w=["pT_d"], lane="st_c")
            tiles2 = [(0, CTX)] + [(CTX + i * 512, 512) for i in range(SEQ // 512)]
            if stop_after == "S2a":
                tiles2 = tiles2[:2]
            zi = 0
            for ti, (t0, n) in enumerate(tiles2):
                ub = u2[ti % 2]
                pb = pst[ti % 2]
                P.dma(ub[:, :, 0:n], u2T_d[:, :, t0:t0 + n].rearrange("k p t -> p k t"), r=["u2T_d"], w=[ub], lane="ld_x%d" % (ti % 2))
                own_tile = (t0 >= CTX and t0 - CTX < OWN)
                for ci, c0 in enumerate(FM_COLS):
                    halo_tile = (t0 >= CTX and t0 - CTX < OWN + 512)
                    if (ci >= 16 and not own_tile) or (ci < 4 and not halo_tile):
                        continue
                    ps = P.psum()
                    for kc in range(KC):
                        P.mm(ps[:, 0:n], lhsT=winb[:, kc, c0:c0 + 128], rhs=ub[:, kc, 0:n], start=(kc == 0), stop=(kc == KC - 1), r=[winb, ub], w=[ps])
                    if ci % 2 == 0:
                        P.cp(pb[:, ci, 0:n], ps[:, 0:n], r=[ps], w=[pb], eng="scalar")
                    else:
                        P.cp(pb[:, ci, 0:n], ps[:, 0:n], r=[ps], w=[pb], eng="vector")
                P.dma(pv[:, :, pcol(t0):pcol(t0) + n], pb[:, :, 0:n], r=[pb], w=["pT_d"], lane="st_a%d" % (ti % 2))
                for j in range(n // 128):
                    tok = t0 + j * 128
                    zb = zst[zi % 2]
                    bb = bast[zi % 2]
                    if tok >= CTX and tok - CTX < OWN:
                        ps = P.psum()
                        for kc in range(KC):
                            P.mm(ps[:], lhsT=ub[:, kc, j * 128:(j + 1) * 128], rhs=winb[:, kc, 1536:2048], start=(kc == 0), stop=(kc == KC - 1), r=[winb, ub], w=[ps])
                        P.act(zb[:], ps[:], AF.Silu, r=[ps], w=[zb])
                        P.dma(z_d[tok - CTX:tok - CTX + 128, :], zb[:], r=[zb], w=["z_d"], lane="st_b%d" % (zi % 2))
                    ps = P.psum()
                    for kc in range(KC):
                        P.mm(ps[:, 0:16], lhsT=ub[:, kc, j * 128:(j + 1) * 128], rhs=winb[:, kc, 2048:2064], start=(kc == 0), stop=(kc == KC - 1), r=[winb, ub], w=[ps])
                    P.cp(bb[:], ps[:, 0:16], r=[ps], w=[bb])
                    P.dma(ba_d[tok:tok + 128, :], bb[:], r=[bb], w=["ba_d"], lane="st_d%d" % (zi % 2))
                    zi += 1
            if debug and "winb" in debug:
                d = dbg_out("winb", [128, KC, IN_COLS], BF16)
                P.dma(d, winb[:], r=[winb], w=["dbg_winb"], lane="st_c")
                d = dbg_out("u2", [128, KC, 512], BF16)
                P.dma(d, u2[1][:], r=[u2[1]], w=["dbg_u2"], lane="st_c")
            P.finish()
            P.emit()
        if stop_after in ("S2", "S2a"):
            return nc, I, dbg

        with ExitStack() as st:
            P.st = st
            cwr = P.sb("s3p_cwr", [60, 128], F32)
            cw = P.sb("s3p_cw", [128, 60], F32)
            convd = P.sb("s3p_convd", [128, 60, 128], BF16)
            ones = P.sb("s3p_ones", [128, 128], BF16)
            raw = [P.sb("s3p_raw%d" % i, [128, 12, 516], BF16) for i in range(2)]
            sv = [P.sb("s3p_s%d" % c, [128, 512], F32) for c in range(8)]
            sq = [P.sb("s3p_sq%d" % i, [128, 512], BF16) for i in range(2)]
            rn = [P.sb("s3p_rn%d" % i, [128, 512], F32) for i in range(2)]
            ost = [P.sb("s3p_o%d" % i, [128, 12, 512], BF16) for i in range(2)]
            P.dma(cwr[:], gconv, r=[], w=[cwr], lane="ld_a")
            P.memset(ones[:], 1.0, w=[ones])
            ps = P.psum()
            P.tr(ps[:, 0:60], cwr[:], ident[0:60, 0:60], r=[cwr, ident], w=[ps])
            P.cp(cw[:], ps[:, 0:60], r=[ps], w=[cw])
            for i in range(60):
                P.ts(convd[:, i, :], ident[:], cw[:, i:i + 1], ALU.mult, r=[ident, cw], w=[convd], eng="gpsimd")
            qv = qkvT_d.rearrange("c p t -> p c t")
            tiles3 = [(0, CTX)] + [(CTX + i * 512, 512) for i in range(SEQ // 512)]
            if stop_after == "S3pa":
                tiles3 = tiles3[:2]
            for ti, (t0, n) in enumerate(tiles3):
                rb = raw[ti % 2]
                ob = ost[ti % 2]
                P.dma(rb[:, :, 0:n + 4], pv[:, 0:12, pcol(t0) - 2:pcol(t0) + n + 2], r=["pT_d"], w=[rb], lane="ld_x%d" % (ti % 2))
                own_tile = (t0 >= CTX and t0 - CTX < OWN)
                c_lo = 0 if own_tile else 4
                for c in range(c_lo, 12):
                    ps = P.psum()
                    for k in range(5):
                        P.mm(ps[:, 0:n], lhsT=convd[:, c * 5 + k, :], rhs=rb[:, c, k:k + n], start=(k == 0), stop=(k == 4), r=[convd, rb], w=[ps])
                    if c < 8:
                        P.act(sv[c][:, 0:n], ps[:, 0:n], AF.Silu, r=[ps], w=[sv[c]])
                    else:
                        P.act(ob[:, c, 0:n], ps[:, 0:n], AF.Silu, r=[ps], w=[ob])
                for c in range(c_lo, 8):
                    q_ = sq[c % 2]
                    r_ = rn[c % 2]
                    P.tt(q_[:, 0:n], sv[c][:, 0:n], sv[c][:, 0:n], ALU.mult, r=[sv[c]], w=[q_], eng="gpsimd")
                    ps = P.psum()
                    P.mm(ps[:, 0:n], lhsT=ones[:], rhs=q_[:, 0:n], start=True, stop=True, r=[ones, q_], w=[ps])
                    P.act(r_[:, 0:n], ps[:, 0:n], AF.Ln, bias=EPS, r=[ps], w=[r_])
                    if c < 4:
                        P.act(r_[:, 0:n], r_[:, 0:n], AF.Exp, scale=-0.5, bias=float(np.log(128.0 ** -0.5)), r=[r_], w=[r_])
                    else:
                        P.act(r_[:, 0:n], r_[:, 0:n], AF.Exp, scale=-0.5, r=[r_], w=[r_])
                    P.tt(ob[:, c, 0:n], sv[c][:, 0:n], r_[:, 0:n], ALU.mult, r=[sv[c], r_], w=[ob])
                P.dma(qv[:, :, t0:t0 + n], ob[:, :, 0:n], r=[ob], w=["qkvT_d"], lane="st_a%d" % (ti % 2))
            P.finish()
            P.emit()
        if stop_after in ("S3p", "S3pa"):
            return nc, I, dbg

        NTL = TALL // 128
        NOUT = OWN // 128
        with ExitStack() as st:
            P.st = st
            base = {}
            for nm, pat, cm, cmpop in [("UPi", 1, -1, ALU.is_ge), ("LOs", -1, 1, ALU.is_gt), ("LOi", -1, 1, ALU.is_ge), ("UPs", 1, -1, ALU.is_gt)]:
                t = P.sb("s3_" + nm, [128, 128], F32)
                P.memset(t[:], 1.0, w=[t])
                P.op("gpsimd", lambda e, t=t, pat=pat, cm=cm, cmpop=cmpop: e.affine_select(
                    out=t[:], in_=t[:], pattern=[[pat, 128]], compare_op=cmpop, fill=0.0, base=0, channel_multiplier=cm), r=[t], w=[t])
                base[nm] = t
            identb = P.sb("s3_identb", [128, 128], BF16)
            P.cp(identb[:], ident[:], r=[ident], w=[identb], eng="gpsimd")
            ones = P.sb("s3_ones", [128, 128], F32)
            P.memset(ones[:], 1.0, w=[ones])
            OFFs = P.sb("s3_offs", [128, 14, 128], BF16)
            for mi in range(14):
                P.dma(OFFs[:, mi, :], offm[mi], r=[], w=[OFFs], lane="ld_w0", eng="gpsimd")
            nw4 = P.sb("s3_nw4", [128, 4, 128], F32)
            P.dma(nw4[:], bass.AP(gnw.tensor, 0, [[0, 128], [0, 4], [1, 128]]), r=[], w=[nw4], lane="ld_a")
            ba = P.sb("s3_ba", [128, NTL, 16], F32)
            gv = P.sb("s3_gv", [128, NTL, 16], F32)
            P.dma(ba[:], ba_d.rearrange("(n p) c -> p n c", p=128), r=["ba_d"], w=[ba], lane="ld_b")
            P.dma(gv[:], bass.AP(gvec.tensor, 0, [[0, 128], [0, NTL], [1, 16]]), r=[], w=[gv], lane="ld_c")
            if stop_after == "S3g2":
                P.finish()
                P.emit()
                return nc, I, dbg
            GA = {nm: P.sb("s3_g_" + nm, [128, 2, NTL, 4], F32) for nm in
                  ["beta", "g", "gcum", "egc", "egt", "ekd", "neb", "t1", "t2", "t3"]}
            for d in range(2):
                bsl = ba[:, :, d * 4:(d + 1) * 4]
                asl = ba[:, :, 8 + d * 4:8 + (d + 1) * 4]
                t1, t2, t3 = GA["t1"][:, d], GA["t2"][:, d], GA["t3"][:, d]
                P.act(t1, bsl, AF.Exp, scale=-1.0, r=[ba], w=[GA["t1"]])
                P.ts(t1, t1, 1.0, ALU.add, r=[GA["t1"]], w=[GA["t1"]])
                P.op("vector", lambda e, o=GA["beta"][:, d], i=t1: e.reciprocal(out=o, in_=i), r=[GA["t1"]], w=[GA["beta"]])
                P.tt(t2, asl, gv[:, :, 8 + d * 4:8 + (d + 1) * 4], ALU.add, r=[ba, gv], w=[GA["t2"]])
                P.act(t3, t2, AF.Abs, r=[GA["t2"]], w=[GA["t3"]])
                P.act(t3, t3, AF.Exp, scale=-1.0, r=[GA["t3"]], w=[GA["t3"]])
                P.act(t3, t3, AF.Ln, bias=1.0, r=[GA["t3"]], w=[GA["t3"]])
                P.stt(t2, in0=t2, scalar=0.0, in1=t3, op0=ALU.max, op1=ALU.add, r=[GA["t2"], GA["t3"]], w=[GA["t2"]])
                P.act(t3, gv[:, :, d * 4:(d + 1) * 4], AF.Exp, r=[gv], w=[GA["t3"]])
                P.stt(GA["g"][:, d], in0=t2, scalar=-1.0, in1=t3, op0=ALU.mult, op1=ALU.mult, r=[GA["t2"], GA["t3"]], w=[GA["g"]])
            if stop_after == "S3g3":
                P.finish()
                P.emit()
                return nc, I, dbg
            psF = P.psum("a")
            gflat = lambda a, d: a[:, d].rearrange("p n h -> p (n h)")
            P.mm(psF[:, 0:NTL * 4], lhsT=base["UPi"][:], rhs=gflat(GA["g"], 0), start=True, stop=True, r=[base["UPi"], GA["g"]], w=[psF])
            P.mm(psF[:, NTL * 4:NTL * 8], lhsT=base["LOi"][:], rhs=gflat(GA["g"], 1), start=True, stop=True, r=[base["LOi"], GA["g"]], w=[psF])
            psT = P.psum("a")
            P.mm(psT[:, 0:NTL * 8], lhsT=ones[:], rhs=GA["g"][:].rearrange("p d n h -> p (d n h)"), start=True, stop=True, r=[ones, GA["g"]], w=[psT])
            if stop_after == "S3g4":
                P.finish()
                P.emit()
                return nc, I, dbg
            fl = lambda a: a[:].rearrange("p d n h -> p (d n h)")
            P.cp(fl(GA["gcum"]), psF[:, 0:NTL * 8], r=[psF], w=[GA["gcum"]])
            P.act(fl(GA["egc"]), fl(GA["gcum"]), AF.Exp, r=[GA["gcum"]], w=[GA["egc"]])
            P.act(fl(GA["egt"]), psT[:, 0:NTL * 8], AF.Exp, r=[psT], w=[GA["egt"]])
            if stop_after == "S3g5":
                P.finish()
                P.emit()
                return nc, I, dbg
            P.act(fl(GA["t2"]), psT[:, 0:NTL * 8], AF.Identity, r=[psT], w=[GA["t2"]])
            P.tt(fl(GA["t1"]), fl(GA["t2"]), fl(GA["gcum"]), ALU.subtract, r=[GA["t2"], GA["gcum"]], w=[GA["t1"]])
            if stop_after == "S3g7":
                P.finish()
                P.emit()
                return nc, I, dbg
            P.act(fl(GA["ekd"]), fl(GA["t1"]), AF.Exp, r=[GA["t1"]], w=[GA["ekd"]])
            if stop_after == "S3g8":
                P.finish()
                P.emit()
                return nc, I, dbg
            P.stt(fl(GA["neb"]), in0=fl(GA["beta"]), scalar=-1.0, in1=fl(GA["egc"]), op0=ALU.mult, op1=ALU.mult, r=[GA["beta"], GA["egc"]], w=[GA["neb"]])

            if stop_after == "S3g6":
                P.finish()
                P.emit()
                return nc, I, dbg
            if stop_after == "S3g":
                for nm in ["beta", "g", "gcum", "egt", "ekd", "neb"]:
                    dd = dbg_out("g_" + nm, [128, 2, NTL, 4])
                    P.dma(dd, GA[nm][:], r=[GA[nm]], w=["dbg_g_" + nm], lane="st_c")
                P.finish()
                P.emit()
                return nc, I, dbg
            o_acc = P.sb("s3_oacc", [128, NOUT, 4, 128], F32)
            okeys = [("o_acc", lt, h) for lt in range(NOUT) for h in range(4)]
            P.memset(o_acc[:], 0.0, w=okeys)
            ld = [P.sb("s3_ld%d" % i, [128, 12, 128], BF16) for i in range(4)]
            zb = [P.sb("s3_zb%d" % i, [128, 4, 128], F32) for i in range(2)]
            SL = {}
            for nm, dt_ in [("kdec", BF16), ("kb", BF16), ("vb", F32), ("KbT", BF16), ("Ag", F32), ("e", F32), ("eT", F32),
                            ("dec_s", F32), ("decT_s", F32), ("decT_i", F32), ("attnT", BF16), ("XT", BF16)]:
                nsl = 3 if nm in ("kdec", "vb", "attnT", "XT") else 2
                SL[nm] = [P.sb("s3_%s%d" % (nm, i), [128, 4, 128], dt_) for i in range(nsl)]
            Tb = [[P.sb("s3_T%d_%d" % (p_, i), [128, 4, 128], BF16) for i in range(2)] for p_ in range(2)]
            TTb = [[P.sb("s3_TT%d_%d" % (p_, i), [128, 4, 128], BF16) for i in range(2)] for p_ in range(2)]
            NLb = [P.sb("s3_NL%d" % p_, [128, 4, 128], BF16) for p_ in range(2)]
            NLTb = [P.sb("s3_NLT%d" % p_, [128, 4, 128], BF16) for p_ in range(2)]
            NoA = [[P.sb("s3_NoA%d_%d" % (p_, i), [128, 4, 128], BF16) for i in range(2)] for p_ in range(2)]
            NoB = [[P.sb("s3_NoB%d_%d" % (p_, i), [128, 4, 128], BF16) for i in range(2)] for p_ in range(2)]
            Yb = [P.sb("s3_Y%d" % p_, [128, 4, 128], BF16) for p_ in range(2)]
            YT = [P.sb("s3_YT%d" % p_, [128, 4, 128], BF16) for p_ in range(2)]
            S = [P.sb("s3_S%d" % h, [128, 128], F32) for h in range(4)]
            Sb = [P.sb("s3_Sb%d" % h, [128, 128], BF16) for h in range(4)]
            rhs2 = [P.sb("s3_rhs2_%d" % h, [128, 128], BF16) for h in range(4)]
            vn = [P.sb("s3_vn%d" % h, [128, 128], BF16) for h in range(4)]
            sqo = P.sb("s3_sqo", [128, 4, 128], F32)
            ms4 = P.sb("s3_ms4", [128, 4], F32)
            t1o = P.sb("s3_t1o", [128, 4, 128], F32)
            nwz = P.sb("s3_nwz", [128, 4, 128], F32)
            yb = P.sb("s3_yb", [128, 4, 128], BF16)
            yT = [P.sb("s3_yT%d" % i, [128, 4, 128], BF16) for i in range(2)]
            qv = qkvT_d.rearrange("c p t -> p c t")
            ymv = ymixT_d.rearrange("c p t -> p c t")

            def v4(ps, off=0):
                return ps[:, off:off + 512].rearrange("p (h i) -> p h i", h=4)

            def gb(nm, d, ti):
                return GA[nm][:, d, ti, :].unsqueeze(2).broadcast_to([128, 4, 128])

            def load(ti, sl3):
                P.dma(ld[sl3][:], qv[:, :, ti * 128:(ti + 1) * 128], r=["qkvT_d"], w=[ld[sl3]], lane="ld_q%d" % sl3)

            def bh(ap2d):
                return ap2d.unsqueeze(1).broadcast_to([128, 4, 128])

            def prep(ti, d, sl, sl3, need_out, pset):
                qk = ld[sl3]
                A4 = bh(base["UPi"][:]) if d == 0 else bh(base["LOi"][:])
                Bm = base["LOs"] if d == 0 else base["UPs"]
                mL = bh(base["LOs"][:]) if d == 0 else bh(base["UPs"][:])
                mLT = bh(base["UPs"][:]) if d == 0 else bh(base["LOs"][:])
                mAT = bh(base["UPi"][:]) if d == 0 else bh(base["LOi"][:])
                I4 = bh(identb[:])
                kdec, vb, attnT, XTf = [SL[k][sl] for k in ["kdec", "vb", "attnT", "XT"]]
                kb, KbT, Ag, e_, eT, dec_s, decT_s, decT_i = [SL[k][pset] for k in
                    ["kb", "KbT", "Ag", "e", "eT", "dec_s", "decT_s", "decT_i"]]
                NL, NLT = NLb[pset], NLTb[pset]
                NoA_, NoB_ = NoA[pset], NoB[pset]
                Tb_, TTb_ = Tb[pset], TTb[pset]
                YT_, Yb_ = YT[pset], Yb[pset]
                offA = lambda li: bh(OFFs[:, 2 * li + (0 if d == 0 else 1), :])
                offB = lambda li: bh(OFFs[:, 2 * li + (1 if d == 0 else 0), :])
                pb = P.psumb()
                for h in range(4):
                    P.tr(pb[:, h * 128:(h + 1) * 128], qk[:, 4 + h, :], identb[:], r=[qk, identb], w=[pb])
                for h in range(4):
                    P.tr(pb[:, 512 + h * 128:512 + (h + 1) * 128], qk[:, 8 + h, :], identb[:], r=[qk, identb], w=[pb])
                Abase = base["UPi"] if d == 0 else base["LOi"]
                for h in range(4):
                    P.act(Ag[:, h, :], Abase[:], AF.Copy, scale=GA["g"][:, d, ti, h:h + 1], r=[Abase, GA["g"]], w=[Ag])
                P.tt(kb[:], v4(pb), gb("beta", d, ti), ALU.mult, r=[pb, GA["beta"]], w=[kb])
                for h in range(4):
                    P.act(kdec[:, h, :], pb[:, h * 128:(h + 1) * 128], AF.Copy, scale=GA["ekd"][:, d, ti, h:h + 1], r=[pb, GA["ekd"]], w=[kdec])
                for h in range(4):
                    P.act(vb[:, h, :], pb[:, 512 + h * 128:512 + (h + 1) * 128], AF.Copy, scale=GA["beta"][:, d, ti, h:h + 1], r=[pb, GA["beta"]], w=[vb])
                yield
                psD = P.psum("a")
                psDT = P.psum("a")
                for h in range(4):
                    P.mm(psD[:, h * 128:(h + 1) * 128], lhsT=Ag[:, h, :], rhs=Bm[:], start=True, stop=True, r=[Ag, Bm], w=[psD])
                for h in range(4):
                    P.mm(psDT[:, h * 128:(h + 1) * 128], lhsT=Bm[:], rhs=Ag[:, h, :], start=True, stop=True, r=[Ag, Bm], w=[psDT])
                pb2 = P.psumb()
                for h in range(4):
                    P.tr(pb2[:, h * 128:(h + 1) * 128], kb[:, h, :], identb[:], r=[kb, identb], w=[pb2])
                P.act(e_[:], v4(psD), AF.Exp, r=[psD], w=[e_])
                P.act(eT[:], v4(psDT), AF.Exp, r=[psDT], w=[eT])
                P.cp(KbT[:], v4(pb2), r=[pb2], w=[KbT], eng="scalar")
                yield
                P.tt(dec_s[:], e_[:], mL, ALU.mult, r=[e_, base["LOs"], base["UPs"]], w=[dec_s], eng="gpsimd")
                P.tt(decT_s[:], eT[:], mLT, ALU.mult, r=[eT, base["LOs"], base["UPs"]], w=[decT_s], eng="gpsimd")
                if need_out:
                    P.tt(decT_i[:], eT[:], mAT, ALU.mult, r=[eT, base["UPi"], base["LOi"]], w=[decT_i], eng="gpsimd")
                yield
                pk = P.psum("a")
                pkT = P.psum("a")
                for h in range(4):
                    P.mm(pk[:, h * 128:(h + 1) * 128], lhsT=KbT[:, h, :], rhs=qk[:, 4 + h, :], start=True, stop=True, r=[KbT, qk], w=[pk])
                for h in range(4):
                    P.mm(pkT[:, h * 128:(h + 1) * 128], lhsT=qk[:, 4 + h, :], rhs=KbT[:, h, :], start=True, stop=True, r=[KbT, qk], w=[pkT])
                if need_out:
                    pq = P.psum("a")
                    for h in range(4):
                        P.mm(pq[:, h * 128:(h + 1) * 128], lhsT=qk[:, 4 + h, :], rhs=qk[:, h, :], start=True, stop=True, r=[qk], w=[pq])
                P.stt(NL[:], in0=v4(pk), scalar=-1.0, in1=dec_s[:], op0=ALU.mult, op1=ALU.mult, r=[pk, dec_s], w=[NL])
                P.stt(NLT[:], in0=v4(pkT), scalar=-1.0, in1=decT_s[:], op0=ALU.mult, op1=ALU.mult, r=[pkT, decT_s], w=[NLT])
                if need_out:
                    P.tt(attnT[:], v4(pq), decT_i[:], ALU.mult, r=[pq, decT_i], w=[attnT])
                yield
                T, TT = Tb_[0], TTb_[0]
                P.tt(NoA_[0][:], NL[:], offA(0), ALU.mult, r=[NL, OFFs], w=[NoA_[0]], eng="gpsimd")
                P.tt(NoB_[0][:], NLT[:], offB(0), ALU.mult, r=[NLT, OFFs], w=[NoB_[0]], eng="vector")
                P.tt(T[:], NoA_[0][:], I4, ALU.add, r=[NoA_[0], identb], w=[T], eng="gpsimd")
                P.tt(TT[:], NoB_[0][:], I4, ALU.add, r=[NoB_[0], identb], w=[TT], eng="vector")
                P.tt(NoA_[1][:], NL[:], offA(1), ALU.mult, r=[NL, OFFs], w=[NoA_[1]], eng="gpsimd")
                P.tt(NoB_[1][:], NLT[:], offB(1), ALU.mult, r=[NLT, OFFs], w=[NoB_[1]], eng="gpsimd")
                yield
                for li in range(1, 7):
                    last = (li == 6)
                    Tn = Tb_[li % 2]
                    TTn = XTf if last else TTb_[li % 2]
                    nA, nB = NoA_[li % 2], NoB_[li % 2]
                    pY = P.psum("a")
                    for h in range(4):
                        P.mm(pY[:, h * 128:(h + 1) * 128], lhsT=nA[:, h, :], rhs=TT[:, h, :], start=True, stop=True, r=[nA, TT], w=[pY])
                    if not last:
                        pY2 = P.psum("a")
                        for h in range(4):
                            P.mm(pY2[:, h * 128:(h + 1) * 128], lhsT=nB[:, h, :], rhs=T[:, h, :], start=True, stop=True, r=[nB, T], w=[pY2])
                    P.cp(YT_[:], v4(pY), r=[pY], w=[YT_], eng="scalar")
                    if not last:
                        P.cp(Yb_[:], v4(pY2), r=[pY2], w=[Yb_], eng=("scalar" if li % 2 == 0 else "vector"))
                        nA2, nB2 = NoA_[(li + 1) % 2], NoB_[(li + 1) % 2]
                        P.tt(nA2[:], NL[:], offA(li + 1), ALU.mult, r=[NL, OFFs], w=[nA2], eng="gpsimd")
                        if li + 1 < 6:
                            P.tt(nB2[:], NLT[:], offB(li + 1), ALU.mult, r=[NLT, OFFs], w=[nB2], eng=("gpsimd" if li % 2 else "vector"))
                    yield
                    pZ = P.psum("a")
                    for h in range(4):
                        P.mm(pZ[:, h * 128:(h + 1) * 128], lhsT=identb[:], rhs=TT[:, h, :], start=True, stop=False, r=[identb, TT], w=[pZ])
                        P.mm(pZ[:, h * 128:(h + 1) * 128], lhsT=T[:, h, :], rhs=YT_[:, h, :], start=False, stop=True, r=[T, YT_], w=[pZ])
                    if not last:
                        pZ2 = P.psum("a")
                        for h in range(4):
                            P.mm(pZ2[:, h * 128:(h + 1) * 128], lhsT=identb[:], rhs=T[:, h, :], start=True, stop=False, r=[identb, T], w=[pZ2])
                            P.mm(pZ2[:, h * 128:(h + 1) * 128], lhsT=TT[:, h, :], rhs=Yb_[:, h, :], start=False, stop=True, r=[TT, Yb_], w=[pZ2])
                    P.cp(TTn[:], v4(pZ), r=[pZ], w=[TTn], eng=("scalar" if li % 2 else "vector"))
                    if not last:
                        P.cp(Tn[:], v4(pZ2), r=[pZ2], w=[Tn], eng=("vector" if li % 2 else "scalar"))
                    T, TT = Tn, TTn
                    yield

            def scan(ti, d, sl, sl3, need_out):
                qk = ld[sl3]
                kdec, vb, attnT, XTf = SL["kdec"][sl], SL["vb"][sl], SL["attnT"][sl], SL["XT"][sl]
                lt = ti - 2
                gcol_ = lambda nm, h: GA[nm][:, d, ti, h:h + 1]
                pas, pvs, pds = {}, {}, {}
                for hp in range(2):
                    hs = [2 * hp, 2 * hp + 1]
                    pa = P.psum("b")
                    for j, h in enumerate(hs):
                        P.mm(pa[:, j * 128:(j + 1) * 128], lhsT=qk[:, 4 + h, :], rhs=Sb[h][:], start=True, stop=True, r=[qk, Sb[h]], w=[pa])
                    for j, h in enumerate(hs):
                        P.stt(rhs2[h][:], in0=pa[:, j * 128:(j + 1) * 128], scalar=gcol_("neb", h), in1=vb[:, h, :], op0=ALU.mult, op1=ALU.add,
                              r=[pa, GA["neb"], vb], w=[rhs2[h]])
                    yield
                    pv_ = P.psum("b")
                    for j, h in enumerate(hs):
                        P.mm(pv_[:, j * 128:(j + 1) * 128], lhsT=XTf[:, h, :], rhs=rhs2[h][:], start=True, stop=True, r=[XTf, rhs2[h]], w=[pv_])
                    for j, h in enumerate(hs):
                        P.cp(vn[h][:], pv_[:, j * 128:(j + 1) * 128], r=[pv_], w=[vn[h]], eng="scalar")
                    yield
                    if need_out:
                        pc = P.psum("b")
                        for j, h in enumerate(hs):
                            P.mm(pc[:, j * 256:j * 256 + 128], lhsT=qk[:, h, :], rhs=Sb[h][:], start=True, stop=True, r=[qk, Sb[h]], w=[pc])
                            P.mm(pc[:, j * 256 + 128:j * 256 + 256], lhsT=attnT[:, h, :], rhs=vn[h][:], start=True, stop=True, r=[attnT, vn[h]], w=[pc])
                        for j, h in enumerate(hs):
                            ok = ("o_acc", lt, h)
                            P.stt(o_acc[:, lt, h, :], in0=pc[:, j * 256:j * 256 + 128], scalar=gcol_("egc", h), in1=o_acc[:, lt, h, :], op0=ALU.mult, op1=ALU.add,
                                  r=[pc, GA["egc"], ok], w=[ok])
                            P.tt(o_acc[:, lt, h, :], o_acc[:, lt, h, :], pc[:, j * 256 + 128:j * 256 + 256], ALU.add, r=[pc, ok], w=[ok])
                        yield
                    pd = P.psum("b")
                    for j, h in enumerate(hs):
                        P.mm(pd[:, j * 128:(j + 1) * 128], lhsT=kdec[:, h, :], rhs=vn[h][:], start=True, stop=True, r=[kdec, vn[h]], w=[pd])
                    for j, h in enumerate(hs):
                        P.stt(S[h][:], in0=S[h][:], scalar=gcol_("egt", h), in1=pd[:, j * 128:(j + 1) * 128], op0=ALU.mult, op1=ALU.add, r=[S[h], GA["egt"], pd], w=[S[h]])
                        P.cp(Sb[h][:], S[h][:], r=[S[h]], w=[Sb[h]], eng="scalar")
                    yield

            def finalize(lt, zi):
                z_ = zb[zi % 2]
                y_ = yT[zi % 2]
                oks = [("o_acc", lt, h) for h in range(4)]
                P.tt(sqo[:], o_acc[:, lt], o_acc[:, lt], ALU.mult, r=oks, w=[sqo], eng="gpsimd")
                P.tt(nwz[:], z_[:], nw4[:], ALU.mult, r=[z_, nw4], w=[nwz], eng="gpsimd")
                yield
                P.op("vector", lambda e: e.tensor_reduce(out=ms4[:], in_=sqo[:], axis=mybir.AxisListType.X, op=ALU.add), r=[sqo], w=[ms4])
                yield
                P.act(ms4[:], ms4[:], AF.Ln, scale=1.0 / 128, bias=EPS, r=[ms4], w=[ms4])
                P.act(ms4[:], ms4[:], AF.Exp, scale=-0.5, r=[ms4], w=[ms4])
                yield
                P.tt(t1o[:], o_acc[:, lt], ms4[:].unsqueeze(2).broadcast_to([128, 4, 128]), ALU.mult, r=oks + [ms4], w=[t1o])
                P.tt(yb[:], t1o[:], nwz[:], ALU.mult, r=[t1o, nwz], w=[yb])
                yield
                pbf = P.psumb()
                for h in range(4):
                    P.tr(pbf[:, h * 128:(h + 1) * 128], yb[:, h, :], identb[:], r=[yb, identb], w=[pbf])
                P.cp(y_[:], v4(pbf), r=[pbf], w=[y_], eng="scalar")
                yield
                P.dma(ymv[:, 0:4, lt * 128:(lt + 1) * 128], y_[:], r=[y_], w=["ymixT_d"], lane="st_a%d" % (zi % 2))

            def advance(g, n=None):
                try:
                    next(g)
                    return True
                except StopIteration:
                    return False

            orders = [(0, [0, 1] + list(range(2, 2 + NOUT))), (1, [1, 0] + list(range(NTL - 1, 1, -1)))]
            HALF = 8
            gi = 0
            zi = 0
            pending_fin = None
            nd = lambda ti: (2 <= ti < 2 + NOUT)
            for d, order in orders:
                for h in range(4):
                    P.memset(S[h][:], 0.0, w=[S[h]])
                    P.memset(Sb[h][:], 0.0, w=[Sb[h]])
                n_o = len(order)
                g0 = gi
                mk = lambda i: prep(order[i], d, (g0 + i) % 3, (g0 + i) % 4, nd(order[i]), (g0 + i) % 2)
                for i in range(min(3, n_o)):
                    load(order[i], (g0 + i) % 4)
                gold = mk(0)
                while advance(gold):
                    pass
                gold = mk(1) if n_o > 1 else None
                if gold is not None:
                    for _ in range(HALF):
                        advance(gold)
                for i, ti in enumerate(order):
                    if i + 3 < n_o:
                        load(order[i + 3], (g0 + i + 3) % 4)
                    if d == 1 and nd(ti):
                        P.dma(zb[zi % 2][:].rearrange("p h i -> p (h i)"), z_d[(ti - 2) * 128:(ti - 1) * 128, :], r=["z_d"], w=[zb[zi % 2]], lane="ld_z%d" % (zi % 2))
                    gnew = mk(i + 2) if i + 2 < n_o else None
                    gsc = scan(ti, d, (g0 + i) % 3, (g0 + i) % 4, nd(ti))
                    act_ = {"sc": gsc, "old": gold, "new": gnew, "fin": pending_fin}
                    nnew = 0
                    while any(v is not None for k, v in act_.items() if k != "new") or (act_["new"] is not None and nnew < HALF):
                        for k in ["sc", "old", "new", "fin"]:
                            g = act_[k]
                            if g is None:
                                continue
                            if k == "new":
                                if nnew >= HALF:
                                    continue
                                nnew += 1
                            if not advance(g):
                                act_[k] = None
                                if k == "new":
                                    gnew = None
                    pending_fin = None
                    if d == 1 and nd(ti):
                        pending_fin = finalize(ti - 2, zi)
                        zi += 1
                    gold = gnew
                gi += n_o
            while pending_fin is not None and advance(pending_fin):
                pass
            if debug and "o_acc" in debug:
                dd = dbg_out("o_acc", [128, NOUT, 4, 128])
                P.dma(dd, o_acc[:], r=okeys, w=["dbg_o_acc"], lane="st_c")
                for nm in ["beta", "g", "gcum", "egt"]:
                    dd = dbg_out("g_" + nm, [128, 2, NTL, 4])
                    P.dma(dd, GA[nm][:], r=[GA[nm]], w=["dbg_g_" + nm], lane="st_c")
            P.finish()
            P.emit()
        if stop_after in ("S3", "S3a"):
            return nc, I, dbg

        with ExitStack() as st:
            P.st = st
            lcr = P.sb("s4_lcr", [24, 128], F32)
            lvr = P.sb("s4_lvr", [24, 128], F32)
            lc = P.sb("s4_lc", [128, 24], F32)
            lv = P.sb("s4_lv", [128, 24], F32)
            cd = P.sb("s4_cd", [128, 20, 128], BF16)
            Wg = P.sb("s4_Wg", [128, 16, 128], BF16)
            sc1 = P.sb("s4_sc1", [128, 8], F32)
            sc2 = P.sb("s4_sc2", [128, 8], F32)
            P.dma(lcr[:], lconv, r=[], w=[lcr], lane="ld_a")
            P.dma(lvr[:], lvec, r=[], w=[lvr], lane="ld_b")
            ps = P.psum()
            P.tr(ps[:, 0:24], lcr[:], ident[0:24, 0:24], r=[lcr, ident], w=[ps])
            P.cp(lc[:], ps[:, 0:24], r=[ps], w=[lc])
            ps = P.psum()
            P.tr(ps[:, 0:24], lvr[:], ident[0:24, 0:24], r=[lvr, ident], w=[ps])
            P.cp(lv[:], ps[:, 0:24], r=[ps], w=[lv])
            for i in range(20):
                P.ts(cd[:, i, :], ident[:], lc[:, i:i + 1], ALU.mult, r=[ident, lc], w=[cd], eng="gpsimd")
            P.memset(Wg[:], 0.0, w=[Wg])
            for d in range(2):
                for g in range(2):
                    for c in range(4):
                        for nb in range(2):
                            P.dma(Wg[nb * 64:(nb + 1) * 64, d * 8 + g * 4 + c, nb * 64:(nb + 1) * 64], lwg[d, g, 2 * c + nb],
                                  r=[], w=[Wg], lane="ld_w0", eng="gpsimd")
            P.act(sc1[:], lv[:, 16:24], AF.Exp, scale=-1.0, r=[lv], w=[sc1])
            P.act(sc1[:], sc1[:], AF.Ln, bias=1.0, r=[sc1], w=[sc1])
            P.ts(sc2[:], sc1[:], -16.0, ALU.mult, r=[sc1], w=[sc2])
            P.ts(sc1[:], sc1[:], -8.0, ALU.mult, r=[sc1], w=[sc1])
            if stop_after == "S4a":
                P.finish()
                P.emit()
                return nc, I, dbg
            raw_l = P.sb("s4_rawl", [128, SEQ], BF16)
            xcm = P.sb("s4_xcm", [128, SEQ + 4], BF16)
            xcp = P.sb("s4_xcp", [128, CTX + 4], BF16)
            xc = P.sb("s4_xc", [128, TALL], F32)
            xcb = P.sb("s4_xcb", [128, TALL], BF16)
            rr = P.sb("s4_rr", [128, TALL], F32)
            ii = P.sb("s4_ii", [128, TALL], F32)
            aa = P.sb("s4_aa", [128, TALL], F32)
            hd = [P.sb("s4_h%d" % d, [128, TALL], F32) for d in range(2)]
            rg = P.sb("s4_rg", [128, OWN], BF16)
            g1 = P.sb("s4_g1", [128, OWN], F32)
            g2 = P.sb("s4_g2", [128, OWN], F32)
            yl = P.sb("s4_yl", [128, OWN], BF16)
            P.memset(xcm[:, 0:2], 0.0, w=[xcm])
            P.op("gpsimd", lambda e: e.memset(xcm[:, SEQ + 2:SEQ + 4], 0.0), r=[xcm], w=[xcm])
            P.memset(xcp[:], 0.0, w=[xcp])
            blocks = [(0, CTX)] + [(CTX + i * 512, 512) for i in range(SEQ // 512)]
            def s4_front(c):
                P.dma(raw_l[:], pT_d[12 + c, :, 6 + CTX:6 + CTX + SEQ], r=["pT_d"], w=[raw_l], lane="ld_x0")
                P.dma(xcp[:, 2:2 + CTX], pT_d[12 + c, :, 2:2 + CTX], r=["pT_d"], w=[xcp], lane="ld_x1")
                P.cp(xcm[:, 2:2 + SEQ].rearrange("p (c r) -> p c r", r=64), raw_l[:].rearrange("p (r c) -> p c r", c=64),
                     r=[raw_l], w=[xcm], eng="vector")
                for bi, (b0, n) in enumerate(blocks):
                    ps = P.psum()
                    src = xcp if bi == 0 else xcm
                    o0 = 0 if bi == 0 else b0 - CTX
                    for k in range(5):
                        P.mm(ps[:, 0:n], lhsT=cd[:, c * 5 + k, :], rhs=src[:, o0 + k:o0 + k + n], start=(k == 0), stop=(k == 4), r=[cd, src], w=[ps])
                    P.act(xc[:, b0:b0 + n], ps[:, 0:n], AF.Identity, bias=lc[:, 20 + c:21 + c], r=[ps, lc], w=[xc])
                    P.cp(xcb[:, b0:b0 + n], xc[:, b0:b0 + n], r=[xc], w=[xcb])

            s4_front(0)
            for c in range(4):
                P.dma(rg[:], pT_d[16 + c, :, 6 + CTX:6 + CTX + OWN], r=["pT_d"], w=[rg], lane="ld_z0")
                for d in range(2):
                    wi = d * 8
                    for bi, (b0, n) in enumerate(blocks):
                        pr = P.psum()
                        pi_ = P.psum()
                        P.mm(pr[:, 0:n], lhsT=Wg[:, wi + c, :], rhs=xcb[:, b0:b0 + n], start=True, stop=True, r=[Wg, xcb], w=[pr])
                        P.mm(pi_[:, 0:n], lhsT=Wg[:, wi + 4 + c, :], rhs=xcb[:, b0:b0 + n], start=True, stop=True, r=[Wg, xcb], w=[pi_])
                        P.act(rr[:, b0:b0 + n], pr[:, 0:n], AF.Sigmoid, bias=lv[:, wi + c:wi + c + 1], r=[pr, lv], w=[rr])
                        P.act(ii[:, b0:b0 + n], pi_[:, 0:n], AF.Sigmoid, bias=lv[:, wi + 4 + c:wi + 4 + c + 1], r=[pi_, lv], w=[ii])
                    sci = d * 4 + c
                    P.tt(ii[:], ii[:], xc[:], ALU.mult, r=[ii, xc], w=[ii])
                    if d == 1 and c + 1 < 4:
                        s4_front(c + 1)
                    P.act(aa[:], rr[:], AF.Exp, scale=sc1[:, sci:sci + 1], r=[rr, sc1], w=[aa])
                    P.act(rr[:], rr[:], AF.Exp, scale=sc2[:, sci:sci + 1], r=[rr, sc2], w=[rr])
                    P.act(rr[:], rr[:], AF.Ln, scale=-1.0, bias=1.0, r=[rr], w=[rr])
                    P.act(rr[:], rr[:], AF.Exp, scale=0.5, r=[rr], w=[rr])
                    first = 0 if d == 0 else CTX - 1
                    P.op("gpsimd", lambda e, first=first: e.memset(rr[:, first:first + 1], 1.0), r=[rr], w=[rr])
                    P.tt(ii[:], ii[:], rr[:], ALU.mult, r=[ii, rr], w=[ii])
                    if stop_after == "S4c":
                        P.finish()
                        P.emit()
                        return nc, I, dbg
                    h_ = hd[d]
                    if d == 0:
                        P.op("vector", lambda e, h_=h_: e.tensor_tensor_scan(out=h_[:, 0:CTX], data0=aa[:, 0:CTX], data1=ii[:, 0:CTX],
                                                                             initial=0.0, op0=ALU.mult, op1=ALU.add), r=[aa, ii], w=[h_])
                        P.op("vector", lambda e, h_=h_: e.tensor_tensor_scan(out=h_[:, CTX:TALL], data0=aa[:, CTX:TALL], data1=ii[:, CTX:TALL],
                                                                             initial=h_[:, CTX - 1:CTX], op0=ALU.mult, op1=ALU.add), r=[aa, ii, h_], w=[h_])
                    else:
                        P.op("vector", lambda e, h_=h_: e.tensor_tensor_scan(out=h_[:, 0:CTX][:, ::-1], data0=aa[:, 0:CTX][:, ::-1], data1=ii[:, 0:CTX][:, ::-1],
                                                                             initial=0.0, op0=ALU.mult, op1=ALU.add), r=[aa, ii], w=[h_])
                        P.op("vector", lambda e, h_=h_: e.tensor_tensor_scan(out=h_[:, CTX:TALL][:, ::-1], data0=aa[:, CTX:TALL][:, ::-1], data1=ii[:, CTX:TALL][:, ::-1],
                                                                             initial=h_[:, 0:1], op0=ALU.mult, op1=ALU.add), r=[aa, ii, h_], w=[h_])
                if stop_after == "S4d":
                    P.finish()
                    P.emit()
                    return nc, I, dbg
                P.tt(hd[0][:, CTX:TALL], hd[0][:, CTX:TALL], hd[1][:, CTX:TALL], ALU.add, r=[hd[0], hd[1]], w=[hd[0]])
                P.tt(g1[:], rg[:], rg[:], ALU.mult, r=[rg], w=[g1], eng="gpsimd")
                P.ts(g1[:], g1[:], 0.044715, ALU.mult, s2=1.0, op1=ALU.add, r=[g1], w=[g1])
                P.tt(g1[:], g1[:], rg[:], ALU.mult, r=[g1, rg], w=[g1])
                P.act(g1[:], g1[:], AF.Sigmoid, scale=1.5957691216057308, r=[g1], w=[g1])
                P.tt(g2[:], g1[:], rg[:], ALU.mult, r=[g1, rg], w=[g2], eng="gpsimd")
                hv = hd[0][:, CTX:TALL].rearrange("p (c r) -> p r c", r=64)[:, 0:OWN // 64, :]
                P.tt(yl[:].rearrange("p (r c) -> p r c", c=64), g2[:].rearrange("p (r c) -> p r c", c=64), hv, ALU.mult, r=[g2, hd[0]], w=[yl])
                P.dma(ymixT_d[4 + c], yl[:], r=[yl], w=["ymixT_d"], lane="st_b0")
            P.finish()
            P.emit()
        if stop_after == "S4":
            return nc, I, dbg

        with ExitStack() as st:
            P.st = st
            w1b = P.sb("s6_w1b", [128, KC, FF], BF16)
            w3b = P.sb("s6_w3b", [128, KC, FF], BF16)
            w2b = P.sb("s6_w2b", [128, FC, D], BF16)
            st_a = ExitStack()
            P.st = st_a
            woutb = P.sb("s5_wout", [128, KC, D], BF16)
            wov = w_out.rearrange("(kc p) n -> p kc n", p=128)
            for kc in range(KC):
                P.dma(woutb[:, kc, :], wov[:, kc, :], r=[], w=[woutb], lane="ld_w3", eng="gpsimd")
            load_ffn_weights(1, w1b, w3b, w2b)
            ymt = [P.sb("s5_ym%d" % i, [128, KC, 128], BF16) for i in range(2)]
            hx = [P.sb("s5_hx%d" % i, [128, D], F32) for i in range(2)]
            tmp5 = P.sb("s5_tmp", [128, 512], F32)
            ymv5 = ymixT_d.rearrange("c p t -> p c t")
            for ti in range(OWN // 128):
                ym_, hx_ = ymt[ti % 2], hx[ti % 2]
                P.dma(ym_[:], ymv5[:, :, ti * 128:(ti + 1) * 128], r=["ymixT_d"], w=[ym_], lane="ld_x%d" % (ti % 2))
                P.dma(hx_[:], h1_d[ti * 128:(ti + 1) * 128, :], r=["h1_d"], w=[hx_], lane="ld_z%d" % (ti % 2))
                for hh in range(2):
                    po = P.psum()
                    for kc in range(KC):
                        P.mm(po[:], lhsT=ym_[:, kc, :], rhs=woutb[:, kc, hh * 512:(hh + 1) * 512], start=(kc == 0), stop=(kc == KC - 1), r=[ym_, woutb], w=[po])
                    P.tt(tmp5[:], po[:], gate_b[:, 2, hh * 512:(hh + 1) * 512], ALU.mult, r=[po, gate_b], w=[tmp5])
                    P.tt(hx_[:, hh * 512:(hh + 1) * 512], hx_[:, hh * 512:(hh + 1) * 512], tmp5[:], ALU.add, r=[hx_, tmp5], w=[hx_])
                P.dma(h2_d[ti * 128:(ti + 1) * 128, :], hx_[:], r=[hx_], w=["h2_d"], lane="st_a%d" % (ti % 2))
            P.finish(skip=("ld_w0", "ld_w1", "ld_w2"))
            P.emit()
            st_a.close()
            P.st = st
            if stop_after == "S5a":
                return nc, I, dbg

            nsub = NT // 128
            xss6 = [[P.sb("s6_x%d_%d" % (q_, j), [128, D], F32) for j in range(nsub)] for q_ in range(2)]
            xn = [P.sb("s6_xn%d" % j, [128, D], F32) for j in range(nsub)]
            ssq2 = P.sb("s6_ssq2", [128, 4], F32)
            rstd2 = P.sb("s6_rstd2", [128, 4], F32)
            uT = P.sb("s6_uT", [128, KC, NT], BF16)
            hT = P.sb("s6_hT", [128, FC, NT], BF16)
            sg = [P.sb("s6_sg%d" % j, [128, NT], F32) for j in range(2)]
            tmp = P.sb("s6_tmp", [128, 512], F32)
            ssq = P.sb("s6_ssq", [128, 4], F32)
            rstd = P.sb("s6_rstd", [128, 4], F32)
            fgb = gate_b[:, 0, :]
            P.dma(fgb, bass.AP(final_g.tensor, 0, [[0, 128], [1, D]]), r=[], w=[gate_b], lane="ld_a")

            def s6_load(ti_):
                for j in range(nsub):
                    P.dma(xss6[ti_ % 2][j][:], h2_d[ti_ * NT + j * 128:ti_ * NT + (j + 1) * 128, :], r=["h2_d"], w=[xss6[ti_ % 2][j]],
                          lane="ld_x%d_%d" % (ti_ % 2, j))
            n6 = OWN // NT
            s6_load(0)
            norm_mod_T(xss6[0], nsub, 2, 0, xn, uT, ssq, rstd)

            def fin_gen(xs_, t0_, load_ti):
                if xs_ is not None:
                    for j in range(nsub):
                        P.act(xn[j][:], xs_[j][:], AF.Square, accum_out=ssq2[:, j:j + 1], r=[xs_[j]], w=[xn[j], ssq2])
                    P.act(rstd2[:, 0:nsub], ssq2[:, 0:nsub], AF.Ln, scale=1.0 / D, bias=EPS, r=[ssq2], w=[rstd2])
                    P.act(rstd2[:, 0:nsub], rstd2[:, 0:nsub], AF.Exp, scale=-0.5, r=[rstd2], w=[rstd2])
                    for j in range(nsub):
                        P.stt(xn[j][:], in0=xs_[j][:], scalar=rstd2[:, j:j + 1], in1=fgb, op0=ALU.mult, op1=ALU.mult, r=[xs_[j], rstd2, gate_b], w=[xn[j]])
                        P.dma(y[t0_ + j * 128:t0_ + (j + 1) * 128, :], xn[j][:], r=[xn[j]], w=["y"], lane="st_a%d" % j)
                if load_ti is not None:
                    s6_load(load_ti)
                yield

            pend = None
            for ti in range(n6):
                t0 = ti * NT
                xs = xss6[ti % 2]
                nxt = ti + 1 if ti + 1 < n6 else None
                hook = norm_gen(xss6[(ti + 1) % 2], nsub, 2, 0, xn, uT, ssq, rstd) if nxt is not None else None
                h0 = fin_gen(pend[0] if pend else None, pend[1] if pend else None, nxt)
                ffn_core(xs, nsub, uT, hT, w1b, w3b, w2b, 3, sg, tmp, hook=hook, hook0=h0)
                if hook is not None:
                    for _ in hook:
                        pass
                pend = (xs, t0)
            for _ in fin_gen(pend[0], pend[1], None):
                pass
            P.finish()
            P.emit()
    return nc, I, dbg


def _offmasks():
    ii = np.arange(128)[:, None]
    jj = np.arange(128)[None, :]
    ms = []
    for li in range(7):
        s_ = 1 << li
        m_ = ((ii // (2 * s_)) == (jj // (2 * s_))) & ((ii % (2 * s_)) >= s_) & ((jj % (2 * s_)) < s_)
        ms.append(m_.astype(np.float32))
        ms.append(m_.T.astype(np.float32))
    return np.ascontiguousarray(np.stack(ms, 0))


OFFM = _offmasks()


def make_in_maps(inputs):
    f = lambda a: np.ascontiguousarray(np.asarray(a, dtype=np.float32))
    maps = []
    for core in range(8):
        b, half = core // 2, core % 2
        xb = f(inputs["x"][b])
        cb = f(inputs["ctx"][b])
        if half == 1:
            xb = f(xb[::-1])
            cb = f(cb[::-1])
        cc = np.concatenate([f(inputs["c"][b]).reshape(8, 128), f(inputs["c_ctx"]).reshape(8, 128)], 0)
        m = {
            "x": xb, "ctx": cb, "cc": f(cc),
            "w_ada": f(inputs["w_ada"][0]), "b_ada": f(inputs["b_ada"][0]).reshape(1, -1),
            "norm_g": f(inputs["norm_g"][0]).reshape(24, 128),
            "w1": f(inputs["ffn_w1"][0]), "w3": f(inputs["ffn_w3"][0]), "w2": f(inputs["ffn_w2"][0]),
            "w_in": f(inputs["w_in"][0]), "w_out": f(inputs["w_out"][0]),
            "final_g": f(inputs["final_norm_g"]),
        }
        rev = (half == 1)
        def taps5(w):
            z = np.zeros((1, w.shape[1]), np.float32)
            return np.concatenate([z, w[::-1]], 0) if rev else np.concatenate([w, z], 0)
        g5 = taps5(f(inputs["gdn_conv_w"][0]))
        m["gconv"] = f(g5.reshape(5, 12, 128).transpose(1, 0, 2).reshape(60, 128))
        al = f(inputs["gdn_a_log"][0]); dtb = f(inputs["gdn_dt_bias"][0])
        if rev:
            al = al[::-1]; dtb = dtb[::-1]
        m["gvec"] = f(np.concatenate([al.reshape(-1), dtb.reshape(-1)]))
        m["gnw"] = f(inputs["gdn_norm_w"][0])
        l5 = taps5(f(inputs["lru_conv_w"][0]))
        m["lconv"] = f(np.concatenate([l5.reshape(5, 4, 128).transpose(1, 0, 2).reshape(20, 128),
                                       f(inputs["lru_conv_b"][0]).reshape(4, 128)], 0))
        lw = f(inputs["lru_w_gate"][0]); lb = f(inputs["lru_b_gate"][0]); ll = f(inputs["lru_lambda"][0])
        if rev:
            lw = lw[::-1]; lb = lb[::-1]; ll = ll[::-1]
            wi = m["w_in"].copy()
            o = 2048
            wi[:, o:o + 4], wi[:, o + 4:o + 8] = m["w_in"][:, o + 4:o + 8], m["w_in"][:, o:o + 4]
            wi[:, o + 8:o + 12], wi[:, o + 12:o + 16] = m["w_in"][:, o + 12:o + 16], m["w_in"][:, o + 8:o + 12]
            m["w_in"] = f(wi)
        m["lwg"] = f(lw)
        m["offm"] = OFFM
        m["lvec"] = f(np.concatenate([lb.reshape(16, 128), ll.reshape(8, 128)], 0))
        maps.append(m)
    return maps


def kernel(**inputs):
    nc, I, dbg = build_nc()
    maps = make_in_maps(inputs)
    maps = [{k: v for k, v in m.items() if k in I} for m in maps]
    res = run_bass_kernel_spmd(nc, maps, core_ids=list(range(8)))
    out = np.zeros((4, SEQ, D), np.float32)
    for core in range(8):
        b, half = core // 2, core % 2
        yy = np.asarray(res.results[core]["y"])
        if half == 0:
            out[b, :OWN] = yy
        else:
            out[b, OWN:] = yy[::-1]
    return out
```
